# Optimizing a Trainium2 kernel written in Bass

```python
import math
import jax, jax.numpy as jnp
from jax import lax
import numpy as np

D_MODEL = 1024
BATCH = 32
SEQ = 2048
DEPTH = 2

A_HEADS = 4
A_HEAD_DIM = 64
B_HEADS = 4
B_HEAD_DIM = 128
B_PATTERNS = ((128, 1), (512, 4), (2048, 16))
C_HEADS = 8
C_NOPE_DIM = 64
C_ROPE_DIM = 32
C_V_DIM = 64
C_Q_LORA = 384
C_KV_LORA = 256
ROPE_THETA = 10000.0
FFN_HIDDEN = -(-8 * D_MODEL // (3 * 256)) * 256
A_QK_W = A_HEADS * 2 * A_HEAD_DIM
A_V_W = A_HEADS * 2 * A_HEAD_DIM
B_W = B_HEADS * B_HEAD_DIM
C_DKV_W = C_KV_LORA + C_ROPE_DIM
C_OUT_W = C_HEADS * C_V_DIM
N_BRANCHES = 3
IN_SPLITS = (A_QK_W, A_QK_W, A_V_W, B_W, B_W, B_W, C_Q_LORA, C_DKV_W, N_BRANCHES * D_MODEL)
IN_WIDTH = A_QK_W * 2 + A_V_W + B_W * 3 + C_Q_LORA + C_DKV_W + N_BRANCHES * D_MODEL
QBLOCK = 128
NORM_EPS = 1e-6
NEG_INF = -1e30

kernel_name = "hybrid_gated_diff_dilated_mla_block"


def _rms_norm(x, g):
    xf = x.astype(jnp.float32)
    y = xf * lax.rsqrt(jnp.mean(xf * xf, axis=-1, keepdims=True) + NORM_EPS)
    return (y * g.astype(jnp.float32)).astype(x.dtype)


def _alibi_slopes():
    n = A_HEADS + B_HEADS
    s = 2.0 ** (-8.0 * jnp.arange(1, n + 1, dtype=jnp.float32) / n)
    return s[0::2], s[1::2]


def _rope(x, pos):
    half = x.shape[-1] // 2
    inv_freq = ROPE_THETA ** (-jnp.arange(half, dtype=jnp.float32) / half)
    ang = pos.astype(jnp.float32)[:, None] * inv_freq[None, :]
    cos, sin = jnp.cos(ang)[:, None, :], jnp.sin(ang)[:, None, :]
    xf = x.astype(jnp.float32)
    x1, x2 = xf[..., :half], xf[..., half:]
    return jnp.concatenate([x1 * cos - x2 * sin, x1 * sin + x2 * cos], axis=-1).astype(x.dtype)


def _block_geometry(start, end):
    qpos = jnp.arange(start, end)
    kpos = jnp.arange(end)
    dist = (qpos[:, None] - kpos[None, :]).astype(jnp.float32)
    return dist >= 0, dist


def _causal_block_sweep(block_fn, seq):
    qb = min(QBLOCK, seq)
    return jnp.concatenate([block_fn(s, min(s + qb, seq)) for s in range(0, seq, qb)], axis=1)


def _diff_attention(q, k, v, lam, lam_init, slopes, gain):
    B, S = q.shape[:2]
    q = q.reshape(B, S, A_HEADS, 2, A_HEAD_DIM)
    k = k.reshape(B, S, A_HEADS, 2, A_HEAD_DIM)
    v = v.reshape(B, S, A_HEADS, 2 * A_HEAD_DIM)
    scale = A_HEAD_DIM ** -0.5

    def block(start, end):
        causal, dist = _block_geometry(start, end)
        s = jnp.einsum('bqhcd,bkhcd->bhcqk', q[:, start:end], k[:, :end]).astype(jnp.float32) * scale
        s = s - slopes[:, None, None, None] * dist
        s = jnp.where(causal, s, NEG_INF)
        p = jax.nn.softmax(s, axis=-1)
        p = p[:, :, 0] - lam * p[:, :, 1]
        return jnp.einsum('bhqk,bkhd->bqhd', p.astype(v.dtype), v[:, :end])

    o = _causal_block_sweep(block, S)
    o = _rms_norm(o, gain) * (1.0 - lam_init)
    return o.reshape(B, S, A_V_W)


def _strided_window_attention(q, k, v, window, dil, slopes):
    B, S, H, D = q.shape
    L = S // dil
    nw = window // dil

    def sub(t):
        return t.reshape(B, L, dil, H, D).transpose(0, 2, 1, 3, 4).reshape(B * dil, L, H, D)

    qs, ks, vs = sub(q), sub(k), sub(v)
    Bp = B * dil
    bq = min(QBLOCK, L)
    nb = L // bq
    n_prev = min(-(-nw // bq), nb - 1)
    kb_len = (n_prev + 1) * bq

    def band(t):
        tp = jnp.pad(t, ((0, 0), (n_prev * bq, 0), (0, 0), (0, 0)))
        views = [tp[:, j * bq: j * bq + L].reshape(Bp, nb, bq, H, D) for j in range(n_prev + 1)]
        return jnp.concatenate(views, axis=2)

    qb = qs.reshape(Bp, nb, bq, H, D)
    kb, vb = band(ks), band(vs)
    qi = jnp.arange(L).reshape(nb, bq)
    ki = (jnp.arange(nb)[:, None] - n_prev) * bq + jnp.arange(kb_len)[None, :]
    delta = qi[:, :, None] - ki[:, None, :]
    valid = (delta >= 0) & (delta <= nw) & (ki[:, None, :] >= 0)
    dist = (delta * dil).astype(jnp.float32)
    scale = D ** -0.5
    s = jnp.einsum('bnqhd,bnkhd->bhnqk', qb, kb).astype(jnp.float32) * scale
    s = s - slopes[:, None, None, None] * dist
    s = jnp.where(valid, s, NEG_INF)
    lse = jax.nn.logsumexp(s, axis=-1)
    p = jnp.exp(s - lse[..., None])
    o = jnp.einsum('bhnqk,bnkhd->bnqhd', p.astype(v.dtype), vb).reshape(B, dil, L, H, D)
    o = o.transpose(0, 2, 1, 3, 4).reshape(B, S, H, D)
    lse = lse.transpose(0, 2, 3, 1).reshape(B, dil, L, H).transpose(0, 2, 1, 3).reshape(B, S, H)
    return o, lse


def _dilated_attention(q, k, v, slopes):
    B, S = q.shape[:2]
    q = q.reshape(B, S, B_HEADS, B_HEAD_DIM)
    k = k.reshape(B, S, B_HEADS, B_HEAD_DIM)
    v = v.reshape(B, S, B_HEADS, B_HEAD_DIM)
    outs, lses = [], []
    for window, dil in B_PATTERNS:
        o, lse = _strided_window_attention(q, k, v, window, dil, slopes)
        outs.append(o)
        lses.append(lse)
    o = jnp.stack(outs, axis=0)
    w = jax.nn.softmax(jnp.stack(lses, axis=0), axis=0)
    o = jnp.sum(w[..., None].astype(o.dtype) * o, axis=0)
    return o.reshape(B, S, B_W)


def _mla(c_q, c_kv_pe, q_norm, w_uq, kv_norm, w_ukv, pos):
    B, S = c_q.shape[:2]
    q = (_rms_norm(c_q, q_norm) @ w_uq).reshape(B, S, C_HEADS, C_NOPE_DIM + C_ROPE_DIM)
    q_nope, q_pe = q[..., :C_NOPE_DIM], _rope(q[..., C_NOPE_DIM:], pos)
    c_kv, k_pe = c_kv_pe[..., :C_KV_LORA], c_kv_pe[..., C_KV_LORA:]
    k_pe = _rope(k_pe[:, :, None, :], pos)[:, :, 0]
    kv = (_rms_norm(c_kv, kv_norm) @ w_ukv).reshape(B, S, C_HEADS, C_NOPE_DIM + C_V_DIM)
    k_nope, v = kv[..., :C_NOPE_DIM], kv[..., C_NOPE_DIM:]
    scale = (C_NOPE_DIM + C_ROPE_DIM) ** -0.5

    def block(start, end):
        causal, _ = _block_geometry(start, end)
        s = (jnp.einsum('bqhd,bkhd->bhqk', q_nope[:, start:end], k_nope[:, :end])
             + jnp.einsum('bqhr,bkr->bhqk', q_pe[:, start:end], k_pe[:, :end])).astype(jnp.float32) * scale
        s = jnp.where(causal, s, NEG_INF)
        p = jax.nn.softmax(s, axis=-1)
        return jnp.einsum('bhqk,bkhd->bqhd', p.astype(v.dtype), v[:, :end])

    o = _causal_block_sweep(block, S)
    return o.reshape(B, S, C_OUT_W)


def _hybrid_layer(x, pos, slopes_a, slopes_b, layer_idx, attn_norm, w_in, diff_lambda, diff_norm,
                  mla_q_norm, mla_w_uq, mla_kv_norm, mla_w_ukv, w_branch_a, w_branch_b, w_branch_c,
                  w_out, ffn_norm, w_ffn_gate, w_ffn_up, w_ffn_down):
    h = _rms_norm(x, attn_norm)
    proj = h @ w_in
    offsets = np.cumsum(IN_SPLITS)[:-1].tolist()
    aq, ak, av, bq, bk, bv, c_q, c_kv_pe, gates = jnp.split(proj, offsets, axis=-1)

    lam_init = 0.8 - 0.6 * math.exp(-0.3 * layer_idx)
    lf = diff_lambda.astype(jnp.float32)
    lam = jnp.exp(jnp.sum(lf[0] * lf[1])) - jnp.exp(jnp.sum(lf[2] * lf[3])) + lam_init

    y_a = _diff_attention(aq, ak, av, lam, lam_init, slopes_a, diff_norm) @ w_branch_a
    y_b = _dilated_attention(bq, bk, bv, slopes_b) @ w_branch_b
    y_c = _mla(c_q, c_kv_pe, mla_q_norm, mla_w_uq, mla_kv_norm, mla_w_ukv, pos) @ w_branch_c
    g_a, g_b, g_c = jnp.split(jax.nn.sigmoid(gates), N_BRANCHES, axis=-1)
    x = x + (g_a * y_a + g_b * y_b + g_c * y_c) @ w_out

    h2 = _rms_norm(x, ffn_norm)
    x = x + (jax.nn.silu(h2 @ w_ffn_gate) * (h2 @ w_ffn_up)) @ w_ffn_down
    return x


def setup_inputs(seed: int = 0) -> dict:
    key = jax.random.key(seed)
    ks = jax.random.split(key, 20)

    def nrm(k, shape, scale):
        return jax.random.normal(k, shape, jnp.float32) * scale

    def gain(k, shape):
        return 1.0 + 0.02 * jax.random.normal(k, shape, jnp.float32)

    return {
        "x": nrm(ks[0], (BATCH, SEQ, D_MODEL), 1.0),
        "attn_norm": gain(ks[1], (DEPTH, D_MODEL)),
        "w_in": nrm(ks[2], (DEPTH, D_MODEL, IN_WIDTH), D_MODEL ** -0.5),
        "diff_lambda": nrm(ks[3], (DEPTH, 4, A_HEAD_DIM), 0.1),
        "diff_norm": gain(ks[4], (DEPTH, 2 * A_HEAD_DIM)),
        "mla_q_norm": gain(ks[5], (DEPTH, C_Q_LORA)),
        "mla_w_uq": nrm(ks[6], (DEPTH, C_Q_LORA, C_HEADS * (C_NOPE_DIM + C_ROPE_DIM)), C_Q_LORA ** -0.5),
        "mla_kv_norm": gain(ks[7], (DEPTH, C_KV_LORA)),
        "mla_w_ukv": nrm(ks[8], (DEPTH, C_KV_LORA, C_HEADS * (C_NOPE_DIM + C_V_DIM)), C_KV_LORA ** -0.5),
        "w_branch_a": nrm(ks[9], (DEPTH, A_V_W, D_MODEL), A_V_W ** -0.5),
        "w_branch_b": nrm(ks[10], (DEPTH, B_W, D_MODEL), B_W ** -0.5),
        "w_branch_c": nrm(ks[11], (DEPTH, C_OUT_W, D_MODEL), C_OUT_W ** -0.5),
        "w_out": nrm(ks[12], (DEPTH, D_MODEL, D_MODEL), D_MODEL ** -0.5),
        "ffn_norm": gain(ks[13], (DEPTH, D_MODEL)),
        "w_ffn_gate": nrm(ks[14], (DEPTH, D_MODEL, FFN_HIDDEN), D_MODEL ** -0.5),
        "w_ffn_up": nrm(ks[15], (DEPTH, D_MODEL, FFN_HIDDEN), D_MODEL ** -0.5),
        "w_ffn_down": nrm(ks[16], (DEPTH, FFN_HIDDEN, D_MODEL), FFN_HIDDEN ** -0.5),
        "final_norm": gain(ks[17], (D_MODEL,)),
    }


def reference(x, attn_norm, w_in, diff_lambda, diff_norm, mla_q_norm, mla_w_uq, mla_kv_norm,
              mla_w_ukv, w_branch_a, w_branch_b, w_branch_c, w_out, ffn_norm, w_ffn_gate,
              w_ffn_up, w_ffn_down, final_norm):
    pos = jnp.arange(x.shape[1], dtype=jnp.int32)
    slopes_a, slopes_b = _alibi_slopes()
    for l in range(DEPTH):
        x = _hybrid_layer(x, pos, slopes_a, slopes_b, l, attn_norm[l], w_in[l], diff_lambda[l],
                          diff_norm[l], mla_q_norm[l], mla_w_uq[l], mla_kv_norm[l], mla_w_ukv[l],
                          w_branch_a[l], w_branch_b[l], w_branch_c[l], w_out[l], ffn_norm[l],
                          w_ffn_gate[l], w_ffn_up[l], w_ffn_down[l])
    return _rms_norm(x, final_norm)
```

```python
import contextlib
import math
import numpy as np
import ml_dtypes
import concourse.bass as bass
import concourse.mybir as mybir
from concourse.bass_utils import run_bass_kernel_spmd

F32 = mybir.dt.float32
BF16 = mybir.dt.bfloat16
AF = mybir.ActivationFunctionType
ALU = mybir.AluOpType
AX = mybir.AxisListType

D = 1024
S = 2048
DEPTH = 2
NCORES = 8
INW = 6816
FFN = 2816
NHC = FFN // 128
EPS = 1e-6
NG = S // 512
NT = S // 128


class Prog:
    ENGS = ("pe", "act", "dve", "pool", "sp")
    SAME_ENG_RAW = ("act", "dve", "pool")

    def __init__(self, nc, stack):
        self.nc = nc
        self.stack = stack
        self.ops = {e: [] for e in self.ENGS}
        self.last_w = {}
        self.readers = {}
        self.seen = {e: {} for e in self.ENGS}
        self.marked = {e: set() for e in self.ENGS}
        self.dma_cnt = {}
        self.dma_sem = {}
        self.eng_sem = {}

    def _need(self, consumer, ev, waits):
        if ev is None:
            return
        kind, key, val = ev
        if kind == "e" and key == consumer:
            return
        if self.seen[consumer].get((kind, key), -1) >= val:
            return
        self.seen[consumer][(kind, key)] = val
        waits.append(ev)
        if kind == "e":
            self.marked[key].add(val)

    def _deps(self, consumer, reads, writes, waits):
        for r in reads:
            w = self.last_w.get(r)
            if w is None:
                continue
            if w[0] == "e" and w[1] == consumer:
                if consumer in self.SAME_ENG_RAW and self.seen[consumer].get(("s", consumer), -1) < w[2]:
                    self.seen[consumer][("s", consumer)] = w[2]
                    waits.append(w)
                    self.marked[consumer].add(w[2])
            else:
                self._need(consumer, w, waits)
        for r in writes:
            self._need(consumer, self.last_w.get(r), waits)
            for ev in self.readers.get(r, {}).values():
                self._need(consumer, ev, waits)

    def _record(self, ev, reads, writes):
        for r in reads:
            self.readers.setdefault(r, {})[(ev[0], ev[1])] = ev
        for r in writes:
            self.last_w[r] = ev
            self.readers[r] = {}

    def op(self, eng, fn, reads=(), writes=()):
        ex = [r for r in reads if isinstance(r, tuple) and r[0] == "pb"]
        if ex:
            reads = [r for r in reads if not (isinstance(r, tuple) and r[0] == "pb")]
            writes = list(writes) + ex
        waits = []
        self._deps(eng, reads, writes, waits)
        idx = len(self.ops[eng])
        self.ops[eng].append(("c", fn, waits, None))
        self._record(("e", eng, idx), reads, writes)

    def dma(self, queue, sem, fn, reads=(), writes=()):
        waits = []
        self._deps(queue, reads, writes, waits)
        cnt = self.dma_cnt.get(sem, 0) + 16
        self.dma_cnt[sem] = cnt
        self.ops[queue].append(("d", fn, waits, sem))
        self._record(("d", sem, cnt), reads, writes)

    def barrier(self):
        evs = []
        for e in self.ENGS:
            n = len(self.ops[e])
            k = n - 1
            while k >= 0 and self.ops[e][k][0] != "c":
                k -= 1
            if k >= 0:
                evs.append(("e", e, k))
        for s, c in self.dma_cnt.items():
            evs.append(("d", s, c))
        for e in self.ENGS:
            waits = []
            for ev in evs:
                self._need(e, ev, waits)
            if waits:
                self.ops[e].append(("w", None, waits, None))

    def emit(self):
        nc = self.nc
        engs = {"pe": nc.tensor, "act": nc.scalar, "dve": nc.vector, "pool": nc.gpsimd, "sp": nc.sync}
        for e in self.ENGS:
            self.eng_sem[e] = self.stack.enter_context(nc.semaphore("s_" + e))
        for s in self.dma_cnt:
            self.dma_sem[s] = self.stack.enter_context(nc.semaphore("d_" + s))
        rank = {}
        for e in self.ENGS:
            rank[e] = {idx: i + 1 for i, idx in enumerate(sorted(self.marked[e]))}
        block = self.stack.enter_context(nc.Block())
        starters = {"pe": block.tensor, "act": block.scalar, "dve": block.vector,
                    "pool": block.gpsimd, "sp": block.sync}

        def body(e):
            def _(engine):
                for i, (kind, fn, waits, sem) in enumerate(self.ops[e]):
                    for (k, key, val) in waits:
                        if k == "e":
                            engine.wait_ge(self.eng_sem[key], rank[key][val])
                        else:
                            engine.wait_ge(self.dma_sem[key], val)
                    if kind == "w":
                        continue
                    inst = fn(engine)
                    if kind == "d":
                        inst.then_inc(self.dma_sem[sem], 16)
                    elif i in rank[e]:
                        inst.then_inc(self.eng_sem[e], 1)
            return _

        for e in self.ENGS:
            if self.ops[e]:
                starters[e](body(e))


def _consts():
    n = 8
    sl = 2.0 ** (-8.0 * np.arange(1, n + 1, dtype=np.float64) / n)
    slopes = np.concatenate([sl[0::2], sl[1::2]])
    ki = np.arange(128, dtype=np.float64)[:, None, None]
    dl = np.arange(16, dtype=np.float64)[None, None, :]
    bias = (slopes[None, :, None] * (ki - 128.0 * dl)).astype(np.float32).reshape(128, 128)
    kk = np.arange(128)[:, None]
    qq = np.arange(128)[None, :]
    masks = np.zeros((128, 7, 128), np.float32)
    masks[:, 0, :] = (qq >= kk)
    for dlt in range(6):
        d = dlt * 128 + qq - kk
        m = ((d >= 0) & (d <= 128)).astype(np.float32) + ((d >= 0) & (d % 4 == 0) & (d <= 512)) \
            + ((d >= 0) & (d % 16 == 0) & (d <= 2048))
        masks[:, 1 + dlt, :] = m
    half = 16
    inv = 10000.0 ** (-np.arange(half, dtype=np.float32) / half)
    ang = np.arange(S, dtype=np.float32)[None, :] * inv[:, None]
    cos = np.cos(ang).astype(np.float32)
    sin = np.sin(ang).astype(np.float32)
    rope = np.zeros((2, 32, S), np.float32)
    rope[0, :16] = cos
    rope[0, 16:] = cos
    rope[1, :16] = -sin
    rope[1, 16:] = sin
    return {
        "c_bias": bias,
        "c_masks": masks.astype(ml_dtypes.bfloat16).reshape(128, 7 * 128),
        "c_rope": rope,
        "c_identb": np.eye(128, dtype=np.float32).astype(ml_dtypes.bfloat16),
        "c_identf": np.eye(128, dtype=np.float32),
    }


W_NAMES = ["attn_norm", "w_in", "diff_lambda", "diff_norm", "mla_q_norm", "mla_w_uq", "mla_kv_norm",
           "mla_w_ukv", "w_branch_a", "w_branch_b", "w_branch_c", "w_out", "ffn_norm", "w_ffn_gate",
           "w_ffn_up", "w_ffn_down", "final_norm"]
W_SHAPES = {
    "attn_norm": [DEPTH, D], "w_in": [DEPTH, D, INW], "diff_lambda": [DEPTH, 4, 64], "diff_norm": [DEPTH, 128],
    "mla_q_norm": [DEPTH, 384], "mla_w_uq": [DEPTH, 384, 768], "mla_kv_norm": [DEPTH, 256],
    "mla_w_ukv": [DEPTH, 256, 1024], "w_branch_a": [DEPTH, 512, D], "w_branch_b": [DEPTH, 512, D],
    "w_branch_c": [DEPTH, 512, D], "w_out": [DEPTH, D, D], "ffn_norm": [DEPTH, D],
    "w_ffn_gate": [DEPTH, D, FFN], "w_ffn_up": [DEPTH, D, FFN], "w_ffn_down": [DEPTH, FFN, D],
    "final_norm": [D],
}


class _Stop(Exception):
    pass


def build_program(nseq, layers=(0, 1), first=True, last=True, stop_at=None):
    nc = bass.Bass("TRN2", target_bir_lowering=False)
    dram = {}
    if first:
        x_in = nc.dram_tensor("x", [nseq, S, D], F32, kind="ExternalInput").ap()
    else:
        x_in = nc.dram_tensor("xT_in", [nseq, 128, 8, S], F32, kind="ExternalInput").ap()
    for n in W_NAMES:
        dram[n] = nc.dram_tensor(n, W_SHAPES[n], F32, kind="ExternalInput").ap()
    c_bias = nc.dram_tensor("c_bias", [128, 128], F32, kind="ExternalInput").ap()
    c_masks = nc.dram_tensor("c_masks", [128, 7 * 128], BF16, kind="ExternalInput").ap()
    c_rope = nc.dram_tensor("c_rope", [2, 32, S], F32, kind="ExternalInput").ap()
    c_identb = nc.dram_tensor("c_identb", [128, 128], BF16, kind="ExternalInput").ap()
    c_identf = nc.dram_tensor("c_identf", [128, 128], F32, kind="ExternalInput").ap()
    if last:
        out = nc.dram_tensor("out", [nseq, S, D], F32, kind="ExternalOutput").ap()
    else:
        out = nc.dram_tensor("xT_out", [nseq, 128, 8, S], F32, kind="ExternalOutput").ap()
    xTs = nc.dram_tensor("xTs", [nseq, 128, 8, S], F32).ap()
    NL = len(layers)
    WS = {
        "A": nc.dram_tensor("ws_A", [NL, 2, 128, 8 * 768], BF16).ap(),
        "B": nc.dram_tensor("ws_B", [NL, 2, 128, 8 * 768], BF16).ap(),
        "Cs": nc.dram_tensor("ws_Cs", [NL, 128, 8 * 832], BF16).ap(),
        "Cu": nc.dram_tensor("ws_Cu", [NL, 4, 128, 1664], BF16).ap(),
        "GB": nc.dram_tensor("ws_GB", [NL, 8, 128, 4608], BF16).ap(),
        "O": nc.dram_tensor("ws_O", [NL, 128, 8 * 1024], BF16).ap(),
        "GU": nc.dram_tensor("ws_GU", [NL, NHC, 128, 2048], BF16).ap(),
        "Dn": nc.dram_tensor("ws_Dn", [NL, 8, 128, NHC * 128], BF16).ap(),
    }

    with contextlib.ExitStack() as st:
        def sb(name, shape, dt):
            return st.enter_context(nc.sbuf_tensor(name, shape, dt))

        P = Prog(nc, st)
        identb = sb("identb", [128, 128], BF16)
        identf = sb("identf", [128, 128], F32)
        biasT = sb("biasT", [128, 128], F32)
        masks = sb("masks", [128, 7, 128], BF16)
        gains = sb("gains", [128, 64], F32)
        lamt = sb("lamt", [128, 16], F32)
        ones_b = sb("ones_b", [128, 2], BF16)
        ones_r = sb("ones_r", [1, 128], F32)
        srow = [sb(f"srow{i}", [1, 512], F32) for i in range(2)]
        R1 = sb("R1", [128, 8 * S], BF16)
        R2 = sb("R2", [128, 12 * S], BF16)
        R3 = sb("R3", [128, 8 * S], BF16)
        R4 = sb("R4", [128, 20480], BF16)
        WB = [sb(f"WB{i}", [128, 8192], BF16) for i in range(2)]
        pb = [st.enter_context(nc.psum_tensor(f"pb{i}", [128, 512], F32)) for i in range(8)]

        hT = R1[:].rearrange("p (c t) -> p c t", c=8)
        oT = R2[:].rearrange("p (c t) -> p c t", c=12)
        actT = R2[:, 0:NHC * 1024].rearrange("p (c t) -> p c t", c=NHC)
        mixT = R3[:].rearrange("p (c t) -> p c t", c=8)
        QT = R3[:, 0:4096].rearrange("p (h t) -> p h t", h=2)
        KT = R3[:, 4096:8192].rearrange("p (h t) -> p h t", h=2)
        VA = R3[:, 8192:8192 + NT * 260].rearrange("p (t n) -> p t n", t=NT)

        def r4f(off, n):
            return R4[:, off:off + n].bitcast(F32)

        def r4b(off, n):
            return R4[:, off:off + n]

        xg = [r4f(0, 8192).rearrange("p (c t) -> p c t", c=8), r4f(8192, 8192).rearrange("p (c t) -> p c t", c=8)]
        T16 = r4b(16384, 4096)

        bank_ctr = [0]

        def nbank():
            b = bank_ctr[0] % 8
            bank_ctr[0] += 1
            return b

        def MM(o, l, r, start, stop, reads, writes):
            P.op("pe", lambda e: e.matmul(o, lhsT=l, rhs=r, start=start, stop=stop), reads, writes)

        def TR(o, i, ident, reads, writes):
            P.op("pe", lambda e: e.transpose(out=o, in_=i, identity=ident), reads, writes)

        def ACT(o, i, func, reads, writes, bias=None, scale=None, accum=None):
            kw = {}
            if bias is not None:
                kw["bias"] = bias
            if scale is not None:
                kw["scale"] = scale
            if accum is not None:
                kw["accum_out"] = accum
            P.op("act", lambda e: e.activation(out=o, in_=i, func=func, **kw), reads, writes)

        def TS(eng, o, i, s1, s2, op0, op1, reads, writes):
            if op1 is None:
                P.op(eng, lambda e: e.tensor_scalar(out=o, in0=i, scalar1=s1, scalar2=None, op0=op0), reads, writes)
            else:
                P.op(eng, lambda e: e.tensor_scalar(out=o, in0=i, scalar1=s1, scalar2=s2, op0=op0, op1=op1), reads, writes)

        def TT(eng, o, a, b, op, reads, writes):
            P.op(eng, lambda e: e.tensor_tensor(out=o, in0=a, in1=b, op=op), reads, writes)

        def STT(o, a, s, b, op0, op1, reads, writes):
            P.op("dve", lambda e: e.scalar_tensor_tensor(out=o, in0=a, scalar=s, in1=b, op0=op0, op1=op1), reads, writes)

        def CP(eng, o, i, reads, writes):
            if eng == "act":
                P.op("act", lambda e: e.copy(out=o, in_=i), reads, writes)
            else:
                P.op(eng, lambda e: e.tensor_copy(out=o, in_=i), reads, writes)

        def DMA(q, sem, o, i, reads, writes, slow=False):
            if slow:
                P.dma(q, sem, lambda e: e.dma_start(out=o, in_=i, allow_slow_non_contiguous=True), reads, writes)
            else:
                P.dma(q, sem, lambda e: e.dma_start(out=o, in_=i), reads, writes)

        ev_ctr = [0]

        def evac_eng():
            ev_ctr[0] += 1
            return "act" if ev_ctr[0] % 2 else "dve"

        def chk(name):
            if stop_at == name:
                raise _Stop()

        try:
            DMA("sp", "c0", identb[:], c_identb, [], ["const"])
            DMA("sp", "c0", identf[:], c_identf, [], ["const"])
            DMA("sp", "c0", biasT[:], c_bias, [], ["const"])
            DMA("sp", "c0", masks[:].rearrange("p a b -> p (a b)"), c_masks, [], ["const"])
            for li, l in enumerate(layers):
                DMA("sp", "c0", gains[:, li * 8:li * 8 + 8], dram["attn_norm"][l].rearrange("(c p) -> p c", p=128), [], ["const"], slow=True)
                DMA("sp", "c0", gains[:, 16 + li * 8:16 + li * 8 + 8], dram["ffn_norm"][l].rearrange("(c p) -> p c", p=128), [], ["const"], slow=True)
                DMA("sp", "c0", gains[:, 40 + li * 3:40 + li * 3 + 3], dram["mla_q_norm"][l].rearrange("(c p) -> p c", p=128), [], ["const"], slow=True)
                DMA("sp", "c0", gains[:, 46 + li * 2:46 + li * 2 + 2], dram["mla_kv_norm"][l].rearrange("(c p) -> p c", p=128), [], ["const"], slow=True)
                DMA("sp", "c0", gains[:, 50 + li:51 + li], dram["diff_norm"][l].rearrange("(c p) -> p c", p=128), [], ["const"], slow=True)
            DMA("sp", "c0", gains[:, 32:40], dram["final_norm"].rearrange("(c p) -> p c", p=128), [], ["const"], slow=True)
            P.op("pool", lambda e: e.memset(ones_b[:], 1.0), [], ["const"])
            P.op("pool", lambda e: e.memset(ones_r[:], 1.0), [], ["const"])
            dlt = r4f(0, 2 * DEPTH * 256)
            DMA("sp", "c0", dlt, dram["diff_lambda"].rearrange("l a b -> (l a b)").partition_broadcast(128), [], ["dl"])
            prodt = r4f(4096, 128)
            for li, l in enumerate(layers):
                lam_init = 0.8 - 0.6 * math.exp(-0.3 * l)
                for k in range(2):
                    a = dlt[:, l * 256 + k * 128:l * 256 + k * 128 + 64]
                    b = dlt[:, l * 256 + k * 128 + 64:l * 256 + k * 128 + 128]
                    TT("dve", prodt, a, b, ALU.mult, ["dl"], ["prodt"])
                    P.op("dve", lambda e, o=lamt[:, li * 4 + 3:li * 4 + 4]: e.tensor_reduce(out=o, in_=prodt, axis=AX.X, op=ALU.add),
                         ["prodt"], ["lamj"])
                    ACT(lamt[:, li * 4 + k:li * 4 + k + 1], lamt[:, li * 4 + 3:li * 4 + 4], AF.Exp, ["lamj"], [("lame", k)])
                TT("dve", lamt[:, li * 4 + 3:li * 4 + 4], lamt[:, li * 4 + 1:li * 4 + 2], lamt[:, li * 4:li * 4 + 1], ALU.subtract,
                   [("lame", 0), ("lame", 1)], ["lamj"])
                TS("dve", lamt[:, li * 4 + 2:li * 4 + 3], lamt[:, li * 4 + 3:li * 4 + 4], -lam_init, None, ALU.add, None, ["lamj"], ["const"])
                TS("dve", gains[:, 52 + li:53 + li], gains[:, 50 + li:51 + li], 1.0 - lam_init, None, ALU.mult, None, ["const"], ["const2"])
            P.barrier()
            chk("consts")

            stg = [R2[:, i * 12288:(i + 1) * 12288].bitcast(F32) for i in range(2)]
            cst = [R3[:, i * 6144:(i + 1) * 6144] for i in range(2)]
            pp_ctr = [0]

            def prep(dst, n, pieces):
                k = pp_ctr[0] % 2
                pp_ctr[0] += 1
                for (vf, src) in pieces:
                    DMA("sp", f"pp{k}", vf(stg[k]), src, [], [("stg", k)])
                eng = ("dve", "pool")[(pp_ctr[0] // 2) % 2]
                CP(eng, cst[k][:, 0:n], stg[k][:, 0:n], [("stg", k)], [("cst", k)])
                DMA("act", f"pq{k}", dst, cst[k][:, 0:n], [("cst", k)], ["wscr"])

            def v3(c, n, lo, hi):
                return lambda t: t[:, 0:c * n].rearrange("p (c n) -> p c n", c=c)[:, :, lo:hi]

            for li, l in enumerate(layers):
                win = dram["w_in"][l]

                def wcols(lo, n):
                    return win[:, lo:lo + n].rearrange("(c p) n -> p c n", p=128)

                for u in range(2):
                    prep(WS["A"][li, u], 6144, [(v3(8, 768, k * 256, (k + 1) * 256), wcols(k * 512 + u * 256, 256)) for k in range(3)])
                    prep(WS["B"][li, u], 6144, [(v3(8, 768, k * 256, (k + 1) * 256), wcols(1536 + k * 512 + u * 256, 256)) for k in range(3)])
                for hfc in range(2):
                    def wch(lo, n, hfc=hfc):
                        return win[hfc * 512:(hfc + 1) * 512, lo:lo + n].rearrange("(c p) n -> p c n", p=128)
                    prep(WS["Cs"][li][:, hfc * 3328:(hfc + 1) * 3328], 3328, [
                        (v3(4, 832, 0, 640), wch(3072, 640)),
                        (v3(4, 832, 640, 736), wch(3072 + 384 + 192, 96)),
                        (v3(4, 832, 736, 800), wch(3072 + 384 + 192, 64)),
                        (v3(4, 832, 800, 816), wch(3072 + 384 + 256 + 16, 16)),
                        (v3(4, 832, 816, 832), wch(3072 + 384 + 256, 16)),
                    ])
                uq = dram["mla_w_uq"][l]
                ukv = dram["mla_w_ukv"][l]
                for u in range(4):
                    pcs = []
                    pcs.append((lambda t: t[:, 0:576].rearrange("p (c n) -> p c n", c=3),
                                uq[:, u * 192:(u + 1) * 192].rearrange("(c p) n -> p c n", p=128)))
                    for hh in range(2):
                        base = (2 * u + hh) * 96
                        pcs.append((lambda t, hh=hh: t[:, 576:1152].rearrange("p (c n) -> p c n", c=3)[:, :, hh * 96:hh * 96 + 64],
                                    uq[:, base:base + 64].rearrange("(c p) n -> p c n", p=128)))
                        pcs.append((lambda t, hh=hh: t[:, 576:1152].rearrange("p (c n) -> p c n", c=3)[:, :, hh * 96 + 64:hh * 96 + 80],
                                    uq[:, base + 80:base + 96].rearrange("(c p) n -> p c n", p=128)))
                        pcs.append((lambda t, hh=hh: t[:, 576:1152].rearrange("p (c n) -> p c n", c=3)[:, :, hh * 96 + 80:hh * 96 + 96],
                                    uq[:, base + 64:base + 80].rearrange("(c p) n -> p c n", p=128)))
                    pcs.append((lambda t: t[:, 1152:1664].rearrange("p (c n) -> p c n", c=2),
                                ukv[:, u * 256:(u + 1) * 256].rearrange("(c p) n -> p c n", p=128)))
                    prep(WS["Cu"][li, u], 1664, pcs)
                for n in range(8):
                    pcs = []
                    for m in range(3):
                        pcs.append((lambda t, m=m: t[:, 0:3072].rearrange("p (c m n) -> p c m n", c=8, m=3)[:, :, m, :],
                                    wcols(3744 + m * 1024 + n * 128, 128)))
                        wbr = dram[("w_branch_a", "w_branch_b", "w_branch_c")[m]][l]
                        pcs.append((lambda t, m=m: t[:, 3072:4608].rearrange("p (c m n) -> p c m n", c=4, m=3)[:, :, m, :],
                                    wbr[:, n * 128:(n + 1) * 128].rearrange("(c p) n -> p c n", p=128)))
                    prep(WS["GB"][li, n], 4608, pcs)
                wo = dram["w_out"][l]
                for hf in range(2):
                    prep(WS["O"][li][:, hf * 4096:(hf + 1) * 4096], 4096,
                         [(lambda t: t[:, 0:4096].rearrange("p (c n) -> p c n", c=4),
                           wo[hf * 512:(hf + 1) * 512, :].rearrange("(c p) n -> p c n", p=128))])
                wg = dram["w_ffn_gate"][l]
                wu = dram["w_ffn_up"][l]
                for hc in range(NHC):
                    prep(WS["GU"][li, hc], 2048, [
                        (lambda t: t[:, 0:2048].rearrange("p (c k n) -> p c k n", c=8, k=2)[:, :, 0, :],
                         wg[:, hc * 128:(hc + 1) * 128].rearrange("(c p) n -> p c n", p=128)),
                        (lambda t: t[:, 0:2048].rearrange("p (c k n) -> p c k n", c=8, k=2)[:, :, 1, :],
                         wu[:, hc * 128:(hc + 1) * 128].rearrange("(c p) n -> p c n", p=128)),
                    ])
                wd = dram["w_ffn_down"][l]
                for n in range(8):
                    prep(WS["Dn"][li, n], NHC * 128, [
                        (lambda t: t[:, 0:NHC * 128].rearrange("p (c n) -> p c n", c=NHC),
                         wd[:, n * 128:(n + 1) * 128].rearrange("(c p) n -> p c n", p=128))])
            P.barrier()
            chk("prepass")

            wslot = [0]

            def load_w(src, n):
                k = wslot[0] % 2
                wslot[0] += 1
                DMA("sp", f"wb{k}", WB[k][:, 0:n], src, ["wscr"], [("WB", k)])
                return k

            def fm_rstd(sq_chunks, nfeat, reads):
                b1 = nbank()
                nck = len(sq_chunks)
                for c, sq in enumerate(sq_chunks):
                    MM(pb[b1][0:1, :], ones_b[:, 0:1], sq, c == 0, c == nck - 1, reads + ["const"], [("pb", b1)])
                k = b1 % 2
                TS("dve", srow[k][:], pb[b1][0:1, :], 1.0 / nfeat, EPS, ALU.mult, ALU.add, [("pb", b1)], [("srow", k)])
                ACT(srow[k][:], srow[k][:], AF.Ln, [("srow", k)], [("srow", k)])
                ACT(srow[k][:], srow[k][:], AF.Exp, [("srow", k)], [("srow", k)], scale=-0.5)
                b2 = nbank()
                MM(pb[b2][:], ones_r[0:1, :], srow[k][:], True, True, [("srow", k), "const"], [("pb", b2)])
                return b2

            for s in range(nseq):
                if first:
                    xin = [r4f(16384 + i * 2048, 2048) for i in range(2)]
                    for t in range(NT):
                        g, tt = divmod(t, 4)
                        k = t % 2
                        DMA("sp", f"xin{k}", xin[k], x_in[s, t * 128:(t + 1) * 128, :], [], [("xin", k)])
                        for hf in range(2):
                            b = nbank()
                            for c4 in range(4):
                                c = hf * 4 + c4
                                TR(pb[b][:, c4 * 128:(c4 + 1) * 128], xin[k][:, c * 128:(c + 1) * 128], identf[:],
                                   [("xin", k), "const"], [("pb", b)])
                            CP(evac_eng(), xg[g % 2][:, hf * 4:hf * 4 + 4, tt * 128:(tt + 1) * 128],
                               pb[b][:].rearrange("p (c t) -> p c t", c=4), [("pb", b)], [("xg", g % 2)])
                        if tt == 3:
                            DMA("sp", f"xst{g % 2}", xTs[s, :, :, g * 512:(g + 1) * 512], xg[g % 2], [("xg", g % 2)], [("xTs", s, g)])
                else:
                    for g in range(NG):
                        DMA("sp", f"xg{g % 2}", xg[g % 2], x_in[s, :, :, g * 512:(g + 1) * 512], [], [("xg", g % 2)])
                        DMA("sp", f"xst{g % 2}", xTs[s, :, :, g * 512:(g + 1) * 512], xg[g % 2], [("xg", g % 2)], [("xTs", s, g)])
                P.barrier()
                chk("stage0")

                for li, l in enumerate(layers):
                    is_last_layer = last and (li == NL - 1)
                    sqv = T16.rearrange("p (c t) -> p c t", c=8)
                    kcs = load_w(WS["Cs"][li], 8 * 832)
                    for g in range(NG):
                        k = g % 2
                        DMA("sp", f"xg{k}", xg[k], xTs[s, :, :, g * 512:(g + 1) * 512], [("xTs", s, g)], [("xg", k)])
                        ACT(sqv, xg[k], AF.Square, [("xg", k)], ["sqv"])
                        b2 = fm_rstd([sqv[:, c, :] for c in range(8)], D, ["sqv"])
                        for c in range(8):
                            STT(hT[:, c, g * 512:(g + 1) * 512], xg[k][:, c, :], gains[:, li * 8 + c:li * 8 + c + 1], pb[b2][:],
                                ALU.mult, ALU.mult, [("xg", k), ("pb", b2), "const"], [("hT", g)])
                    P.barrier()
                    chk("p1")

                    wcs = WB[kcs][:, 0:8 * 832].rearrange("p (c n) -> p c n", c=8)
                    craw = r4f(0, 5 * 1024).rearrange("p (c t) -> p c t", c=5)
                    csq = r4b(10240, 5 * 512).rearrange("p (c t) -> p c t", c=5)
                    ropet = [r4f(12800 + i * 2048, 2048).rearrange("p (a t) -> p a t", a=2) for i in range(2)]
                    rtmp = [r4f(16896 + i * 1024, 1024) for i in range(2)]
                    for g in range(NG):
                        gs = slice(g * 512, (g + 1) * 512)
                        for cc in range(5):
                            b = nbank()
                            for c in range(8):
                                MM(pb[b][:], wcs[:, c, cc * 128:(cc + 1) * 128], hT[:, c, gs], c == 0, c == 7,
                                   [("hT", g), ("WB", kcs)], [("pb", b)])
                            CP("dve", craw[:, cc, :], pb[b][:], [("pb", b)], [("craw", cc)])
                            ACT(csq[:, cc, :], pb[b][:], AF.Square, [("pb", b)], [("csq", cc)])
                        chk("a_proj")
                        for (c0, nch, nfeat, gcol, dst0) in ((0, 3, 384, 40 + li * 3, 0), (3, 2, 256, 46 + li * 2, 3)):
                            b2 = fm_rstd([csq[:, c0 + c, :] for c in range(nch)], nfeat, [("csq", c0 + c) for c in range(nch)])
                            for c in range(nch):
                                STT(oT[:, dst0 + c, gs], craw[:, c0 + c, :], gains[:, gcol + c:gcol + c + 1], pb[b2][:],
                                    ALU.mult, ALU.mult, [("craw", c0 + c), ("pb", b2), "const"], [("oT", dst0 + c, g)])
                        chk("a_norm")
                        rk = g % 2
                        DMA("sp", f"rope{rk}", ropet[rk][64:96, :, :], c_rope[:, :, gs].rearrange("a p t -> p a t"), [], [("rope", rk)])
                        bm = nbank()
                        for c in range(8):
                            MM(pb[bm][0:96, :], wcs[:, c, 640:736], hT[:, c, gs], c == 0, c == 7, [("hT", g), ("WB", kcs)], [("pb", bm)])
                        bs = nbank()
                        for c in range(8):
                            MM(pb[bs][0:96, :], wcs[:, c, 736:832], hT[:, c, gs], c == 0, c == 7, [("hT", g), ("WB", kcs)], [("pb", bs)])
                        chk("a_mm")
                        TT("dve", rtmp[0][64:96, :], pb[bm][64:96, :], ropet[rk][64:96, 0, :], ALU.mult, [("pb", bm), ("rope", rk)], [("rtmp", 0)])
                        TT("dve", rtmp[1][64:96, :], pb[bs][64:96, :], ropet[rk][64:96, 1, :], ALU.mult, [("pb", bs), ("rope", rk)], [("rtmp", 1)])
                        chk("a_tt")
                        TT("pool", oT[64:96, 5, gs], rtmp[0][64:96, :], rtmp[1][64:96, :], ALU.add, [("rtmp", 0), ("rtmp", 1)], [("oT", 5, g)])
                        chk("a_pool")

                    PT = [r4b(18944 + i * 512, 512) for i in range(3)]
                    o_tm = r4b(0, 1024).rearrange("p (q n) -> p q n", q=4)
                    t0 = r4f(1024, 1024).rearrange("p (q n) -> p q n", q=4)
                    t1 = r4f(2048, 1024).rearrange("p (q n) -> p q n", q=4)
                    ofp = r4f(3072, 1024).rearrange("p (q n) -> p q n", q=4)
                    junk = r4f(4096, 256)
                    small = r4f(4352, 64)
                    ropeu = [r4f(4480 + i * 2048, 2048).rearrange("p (a t) -> p a t", a=2) for i in range(2)]
                    rtu = [r4f(8576 + i * 1024, 1024) for i in range(2)]
                    units = [("C", u) for u in range(4)] + [("A", u) for u in range(2)] + [("B", u) for u in range(2)]

                    def unit_src(kind, u):
                        if kind == "C":
                            return WS["Cu"][li, u], 1664
                        return WS[kind][li, u], 6144

                    P.barrier()
                    chk("p2a")
                    knext = load_w(*unit_src(*units[0]))
                    for ui, (kind, u) in enumerate(units):
                        kw = knext
                        if ui + 1 < len(units):
                            knext = load_w(*unit_src(*units[ui + 1]))
                        dv = 64 if kind == "C" else 128
                        dva = dv + 1
                        P.op("pool", lambda e, dva=dva: e.memset(VA[:, :, 0:2 * dva].rearrange("p t (h n) -> p t h n", h=2)[:, :, :, dva - 1:dva], 1.0),
                             [], [("VA", t) for t in range(NT)])
                        if kind in ("A", "B"):
                            wv = WB[kw][:, 0:6144].rearrange("p (c n) -> p c n", c=8)
                            for hh in range(2):
                                for g in range(NG):
                                    gs = slice(g * 512, (g + 1) * 512)
                                    for (dstT, nm, co) in ((QT, "QT", 0), (KT, "KT", 256)):
                                        b = nbank()
                                        for c in range(8):
                                            MM(pb[b][:], wv[:, c, co + hh * 128:co + (hh + 1) * 128], hT[:, c, gs], c == 0, c == 7,
                                               [("hT", g), ("WB", kw)], [("pb", b)])
                                        CP(evac_eng(), dstT[:, hh, gs], pb[b][:], [("pb", b)], [(nm, hh, g)])
                            for t in range(NT):
                                b = nbank()
                                for c in range(8):
                                    MM(pb[b][:, 0:256], hT[:, c, t * 128:(t + 1) * 128], wv[:, c, 512:768], c == 0, c == 7,
                                       [("hT", t // 4), ("WB", kw)], [("pb", b)])
                                CP(evac_eng(), VA[:, t, 0:258].rearrange("p (h n) -> p h n", h=2)[:, :, 0:128],
                                   pb[b][:, 0:256].rearrange("p (h n) -> p h n", h=2), [("pb", b)], [("VA", t)])
                        else:
                            wm = WB[kw][:, 0:576].rearrange("p (c n) -> p c n", c=3)
                            wsw = WB[kw][:, 576:1152].rearrange("p (c n) -> p c n", c=3)
                            wkv = WB[kw][:, 1152:1664].rearrange("p (c n) -> p c n", c=2)
                            for g in range(NG):
                                gs = slice(g * 512, (g + 1) * 512)
                                rk = g % 2
                                DMA("sp", f"ropeu{rk}", ropeu[rk][64:96, :, :], c_rope[:, :, gs].rearrange("a p t -> p a t"), [], [("ropeu", rk)])
                                for hh in range(2):
                                    bm = nbank()
                                    for c in range(3):
                                        MM(pb[bm][0:96, :], wm[:, c, hh * 96:(hh + 1) * 96], oT[:, c, gs], c == 0, c == 2,
                                           [("oT", c, g), ("WB", kw)], [("pb", bm)])
                                    bs = nbank()
                                    for c in range(3):
                                        MM(pb[bs][0:96, :], wsw[:, c, hh * 96:(hh + 1) * 96], oT[:, c, gs], c == 0, c == 2,
                                           [("oT", c, g), ("WB", kw)], [("pb", bs)])
                                    CP("act", QT[0:64, hh, gs], pb[bm][0:64, :], [("pb", bm)], [("QT", hh, g)])
                                    TT("dve", rtu[0][64:96, :], pb[bm][64:96, :], ropeu[rk][64:96, 0, :], ALU.mult, [("pb", bm), ("ropeu", rk)], [("rtu", 0)])
                                    TT("dve", rtu[1][64:96, :], pb[bs][64:96, :], ropeu[rk][64:96, 1, :], ALU.mult, [("pb", bs), ("ropeu", rk)], [("rtu", 1)])
                                    TT("pool", QT[64:96, hh, gs], rtu[0][64:96, :], rtu[1][64:96, :], ALU.add, [("rtu", 0), ("rtu", 1)], [("QT", hh, g)])
                                    bk = nbank()
                                    for c in range(2):
                                        MM(pb[bk][0:64, :], wkv[:, c, hh * 128:hh * 128 + 64], oT[:, 3 + c, gs], c == 0, c == 1,
                                           [("oT", 3 + c, g), ("WB", kw)], [("pb", bk)])
                                    CP("act", KT[0:64, hh, gs], pb[bk][0:64, :], [("pb", bk)], [("KT", hh, g)])
                                    CP("pool", KT[64:96, hh, gs], oT[64:96, 5, gs], [("oT", 5, g)], [("KT", hh, g)])
                            for t in range(NT):
                                b = nbank()
                                for c in range(2):
                                    MM(pb[b][:, 0:128], oT[:, 3 + c, t * 128:(t + 1) * 128],
                                       wkv[:, c, :].rearrange("p (h n) -> p h n", h=2)[:, :, 64:128], c == 0, c == 1,
                                       [("oT", 3 + c, t // 4), ("WB", kw)], [("pb", b)])
                                CP(evac_eng(), VA[:, t, 0:130].rearrange("p (h n) -> p h n", h=2)[:, :, 0:64],
                                   pb[b][:, 0:128].rearrange("p (h n) -> p h n", h=2), [("pb", b)], [("VA", t)])

                        if kind == "A":
                            pheads = [(hh, c) for hh in range(2) for c in range(2)]
                            scale = 64 ** -0.5
                        elif kind == "B":
                            pheads = [(hh, None) for hh in range(2)]
                            scale = 128 ** -0.5
                        else:
                            pheads = [(hh, None) for hh in range(2)]
                            scale = 96 ** -0.5
                        for g in range(NG):
                            for (hh, comp) in pheads:
                                if kind == "A":
                                    rows = slice(comp * 64, comp * 64 + 64)
                                    hd = 2 * u + hh
                                elif kind == "B":
                                    rows = slice(0, 128)
                                    hd = 4 + 2 * u + hh
                                else:
                                    rows = slice(0, 96)
                                    hd = None
                                nj = 4 * g + 4

                                def QK(j):
                                    i0 = max(j, 4 * g)
                                    n = (4 * g + 4 - i0) * 128
                                    MM(pb[j % 2][:, 0:n], KT[rows, hh, j * 128:(j + 1) * 128], QT[rows, hh, i0 * 128:(4 * g + 4) * 128],
                                       True, True, [("KT", hh, j // 4), ("QT", hh, g)], [("pb", j % 2)])

                                QK(0)
                                for j in range(nj):
                                    i0 = max(j, 4 * g)
                                    nq = 4 * g + 4 - i0
                                    sl = j % 3
                                    ps = pb[j % 2]
                                    if kind == "C":
                                        ACT(PT[sl][:, 0:nq * 128], ps[:, 0:nq * 128], AF.Exp, [("pb", j % 2)], [("PT", sl)], scale=scale)
                                        if j >= 4 * g:
                                            TT("pool", PT[sl][:, 0:128], PT[sl][:, 0:128], masks[:, 0, :], ALU.mult, [("PT", sl), "const"], [("PT", sl)])
                                    else:
                                        for i in range(i0, 4 * g + 4):
                                            q0 = (i - i0) * 128
                                            ACT(PT[sl][:, q0:q0 + 128], ps[:, q0:q0 + 128], AF.Exp, [("pb", j % 2)], [("PT", sl)],
                                                bias=biasT[:, hd * 16 + (i - j):hd * 16 + (i - j) + 1], scale=scale)
                                            if kind == "B":
                                                TT("pool" if (i + j) % 2 else "dve", PT[sl][:, q0:q0 + 128], PT[sl][:, q0:q0 + 128],
                                                   masks[:, 1 + min(i - j, 5), :], ALU.mult, [("PT", sl), "const"], [("PT", sl)])
                                            elif i == j:
                                                TT("pool", PT[sl][:, q0:q0 + 128], PT[sl][:, q0:q0 + 128], masks[:, 0, :], ALU.mult,
                                                   [("PT", sl), "const"], [("PT", sl)])
                                    if j + 1 < nj:
                                        QK(j + 1)
                                    for i in range(i0, 4 * g + 4):
                                        q0 = (i - i0) * 128
                                        qi = i - 4 * g
                                        MM(pb[2 + qi][:, 0:dva], PT[sl][:, q0:q0 + 128], VA[:, j, hh * dva:(hh + 1) * dva], j == 0, j == i,
                                           [("PT", sl), ("VA", j)], [("pb", 2 + qi)])
                                for qi in range(4):
                                    po = pb[2 + qi]
                                    P.op("dve", lambda e, o=small[:, qi:qi + 1], i_=po[:, dv:dva]: e.reciprocal(out=o, in_=i_),
                                         [("pb", 2 + qi)], [("rec", qi)])
                                    if kind == "A":
                                        dst = (t0 if comp == 0 else t1)[:, qi, :]
                                        ACT(dst, po[:, 0:dv], AF.Identity, [("pb", 2 + qi), ("rec", qi)], [("tc", comp, qi)], scale=small[:, qi:qi + 1])
                                    else:
                                        ACT(o_tm[:, qi, hh * dv:(hh + 1) * dv], po[:, 0:dv], AF.Identity, [("pb", 2 + qi), ("rec", qi)],
                                            [("otm", qi)], scale=small[:, qi:qi + 1])
                                if kind == "A" and comp == 1:
                                    for qi in range(4):
                                        STT(ofp[:, qi, :], t1[:, qi, :], lamt[:, li * 4 + 2:li * 4 + 3], t0[:, qi, :], ALU.mult, ALU.add,
                                            [("tc", 0, qi), ("tc", 1, qi), "const"], [("ofp", qi)])
                                        ACT(junk, ofp[:, qi, :], AF.Square, [("ofp", qi)], ["junk", ("ssq", qi)], accum=small[:, 4 + qi:5 + qi])
                                    TS("dve", small[:, 8:12], small[:, 4:8], 1.0 / 128, EPS, ALU.mult, ALU.add, [("ssq", q) for q in range(4)], ["rstdA"])
                                    ACT(small[:, 8:12], small[:, 8:12], AF.Ln, ["rstdA"], ["rstdA"])
                                    ACT(small[:, 8:12], small[:, 8:12], AF.Exp, ["rstdA"], ["rstdA"], scale=-0.5)
                                    for qi in range(4):
                                        TS("dve", o_tm[:, qi, hh * 128:(hh + 1) * 128], ofp[:, qi, :], small[:, 8 + qi:9 + qi], None, ALU.mult, None,
                                           [("ofp", qi), "rstdA"], [("otm", qi)])
                            nchk = 1 if kind == "C" else 2
                            for ck in range(nchk):
                                if kind == "C":
                                    chunk = 8 + u
                                elif kind == "A":
                                    chunk = 2 * u + ck
                                else:
                                    chunk = 4 + 2 * u + ck
                                bt = 6 + (ck + g) % 2
                                ptr = pb[bt][:].bitcast(BF16)
                                for qi in range(4):
                                    TR(ptr[:, qi * 128:(qi + 1) * 128], o_tm[:, qi, ck * 128:(ck + 1) * 128], identb[:],
                                       [("otm", qi), "const"], [("pb", bt)])
                                if kind == "A":
                                    TS("dve", oT[:, chunk, g * 512:(g + 1) * 512], ptr[:, 0:512], gains[:, 52 + li:53 + li], None, ALU.mult, None,
                                       [("pb", bt), "const2"], [("oT", chunk, g)])
                                else:
                                    CP("dve", oT[:, chunk, g * 512:(g + 1) * 512], ptr[:, 0:512], [("pb", bt)], [("oT", chunk, g)])

                    P.barrier()
                    chk("units")
                    sg = [r4f(i * 1024, 1024) for i in range(3)]
                    pr = [r4f(3072 + i * 1024, 1024) for i in range(3)]
                    knext = load_w(WS["GB"][li, 0], 4608)
                    for n in range(8):
                        kw = knext
                        if n + 1 < 8:
                            knext = load_w(WS["GB"][li, n + 1], 4608)
                        else:
                            knext = load_w(WS["O"][li], 8192)
                        wgt = WB[kw][:, 0:3072].rearrange("p (c m n) -> p c m n", c=8, m=3)
                        wbr = WB[kw][:, 3072:4608].rearrange("p (c m n) -> p c m n", c=4, m=3)
                        for g in range(NG):
                            gs = slice(g * 512, (g + 1) * 512)
                            bg, by = [], []
                            for m in range(3):
                                b = nbank()
                                for c in range(8):
                                    MM(pb[b][:], wgt[:, c, m, :], hT[:, c, gs], c == 0, c == 7, [("hT", g), ("WB", kw)], [("pb", b)])
                                bg.append(b)
                                b = nbank()
                                for c in range(4):
                                    MM(pb[b][:], wbr[:, c, m, :], oT[:, m * 4 + c, gs], c == 0, c == 3, [("oT", m * 4 + c, g), ("WB", kw)], [("pb", b)])
                                by.append(b)
                            for m in range(3):
                                ACT(sg[m], pb[bg[m]][:], AF.Sigmoid, [("pb", bg[m])], [("sg", m)])
                                TT("dve", pr[m], sg[m], pb[by[m]][:], ALU.mult, [("sg", m), ("pb", by[m])], [("pr", m)])
                            TT("pool", pr[0], pr[0], pr[1], ALU.add, [("pr", 0), ("pr", 1)], [("pr", 0)])
                            TT("pool", mixT[:, n, gs], pr[0], pr[2], ALU.add, [("pr", 0), ("pr", 2)], [("mixT", n, g)])

                    P.barrier()
                    chk("p3a")
                    kw = knext
                    wo_t = WB[kw][:, 0:8192].rearrange("p (c n) -> p c n", c=8)
                    knext = load_w(WS["GU"][li, 0], 2048)
                    h2T = hT
                    for g in range(NG):
                        gs = slice(g * 512, (g + 1) * 512)
                        k = g % 2
                        DMA("sp", f"xg{k}", xg[k], xTs[s, :, :, gs], [("xTs", s, g)], [("xg", k)])
                        for n in range(8):
                            b = nbank()
                            for c in range(8):
                                MM(pb[b][:], wo_t[:, c, n * 128:(n + 1) * 128], mixT[:, c, gs], c == 0, c == 7, [("mixT", c, g), ("WB", kw)], [("pb", b)])
                            TT("dve", xg[k][:, n, :], xg[k][:, n, :], pb[b][:], ALU.add, [("xg", k), ("pb", b)], [("xg", k)])
                        DMA("act", f"xst{k}", xTs[s, :, :, gs], xg[k], [("xg", k)], [("xTs", s, g)])
                        ACT(sqv, xg[k], AF.Square, [("xg", k)], ["sqv"])
                        b2 = fm_rstd([sqv[:, c, :] for c in range(8)], D, ["sqv"])
                        for c in range(8):
                            STT(h2T[:, c, gs], xg[k][:, c, :], gains[:, 16 + li * 8 + c:16 + li * 8 + c + 1], pb[b2][:],
                                ALU.mult, ALU.mult, [("xg", k), ("pb", b2), "const"], [("h2T", g)])

                    P.barrier()
                    chk("p3b")
                    sqv4 = R3[:, 0:4096].rearrange("p (c t) -> p c t", c=8)
                    silu = [R3[:, 4096 + i * 1024:4096 + (i + 1) * 1024].bitcast(F32) for i in range(2)]
                    otl = [R3[:, 6144 + i * 2048:6144 + (i + 1) * 2048].bitcast(F32) for i in range(2)]
                    for hf in range(2):
                        for hc in range(NHC):
                            kw = knext
                            if hc + 1 < NHC:
                                knext = load_w(WS["GU"][li, hc + 1], 2048)
                            else:
                                knext = load_w(WS["Dn"][li, 0], NHC * 128)
                            wgu = WB[kw][:, 0:2048].rearrange("p (c k n) -> p c k n", c=8, k=2)
                            for g2 in range(2):
                                g = hf * 2 + g2
                                gs = slice(g * 512, (g + 1) * 512)
                                bgt = nbank()
                                for c in range(8):
                                    MM(pb[bgt][:], wgu[:, c, 0, :], h2T[:, c, gs], c == 0, c == 7, [("h2T", g), ("WB", kw)], [("pb", bgt)])
                                bup = nbank()
                                for c in range(8):
                                    MM(pb[bup][:], wgu[:, c, 1, :], h2T[:, c, gs], c == 0, c == 7, [("h2T", g), ("WB", kw)], [("pb", bup)])
                                sk = (hc * 2 + g2) % 2
                                ACT(silu[sk], pb[bgt][:], AF.Silu, [("pb", bgt)], [("silu", sk)])
                                TT("dve", actT[:, hc, g2 * 512:(g2 + 1) * 512], silu[sk], pb[bup][:], ALU.mult, [("silu", sk), ("pb", bup)],
                                   [("actT", hc, g2)])
                        for g2 in range(2):
                            g = hf * 2 + g2
                            DMA("sp", f"xg{g2}", xg[g2], xTs[s, :, :, g * 512:(g + 1) * 512], [("xTs", s, g)], [("xg", g2)])
                        for n in range(8):
                            kw = knext
                            if n + 1 < 8:
                                knext = load_w(WS["Dn"][li, n + 1], NHC * 128)
                            elif hf == 0:
                                knext = load_w(WS["GU"][li, 0], 2048)
                            wdn = WB[kw][:, 0:NHC * 128].rearrange("p (c n) -> p c n", c=NHC)
                            for g2 in range(2):
                                b = nbank()
                                for hc in range(NHC):
                                    MM(pb[b][:], wdn[:, hc, :], actT[:, hc, g2 * 512:(g2 + 1) * 512], hc == 0, hc == NHC - 1,
                                       [("actT", hc, g2), ("WB", kw)], [("pb", b)])
                                TT("dve", xg[g2][:, n, :], xg[g2][:, n, :], pb[b][:], ALU.add, [("xg", g2), ("pb", b)], [("xg", g2)])
                        for g2 in range(2):
                            g = hf * 2 + g2
                            gs = slice(g * 512, (g + 1) * 512)
                            if not is_last_layer:
                                if li == NL - 1:
                                    DMA("act", f"xst{g2}", out[s, :, :, gs], xg[g2], [("xg", g2)], [("outT", s, g)])
                                else:
                                    DMA("act", f"xst{g2}", xTs[s, :, :, gs], xg[g2], [("xg", g2)], [("xTs", s, g)])
                            else:
                                ACT(sqv4, xg[g2], AF.Square, [("xg", g2)], ["sqv4"])
                                b2 = fm_rstd([sqv4[:, c, :] for c in range(8)], D, ["sqv4"])
                                for c in range(8):
                                    STT(xg[g2][:, c, :], xg[g2][:, c, :], gains[:, 32 + c:33 + c], pb[b2][:], ALU.mult, ALU.mult,
                                        [("xg", g2), ("pb", b2), "const"], [("xg", g2)])
                                for tt in range(4):
                                    t = g * 4 + tt
                                    ok = tt % 2
                                    otv = otl[ok]
                                    for hf2 in range(2):
                                        b = nbank()
                                        for c4 in range(4):
                                            c = hf2 * 4 + c4
                                            TR(pb[b][:, c4 * 128:(c4 + 1) * 128], xg[g2][:, c, tt * 128:(tt + 1) * 128], identf[:],
                                               [("xg", g2), "const"], [("pb", b)])
                                        CP(evac_eng(), otv[:, hf2 * 512:(hf2 + 1) * 512], pb[b][:], [("pb", b)], [("otl", ok)])
                                    DMA("sp", f"ost{ok}", out[s, t * 128:(t + 1) * 128, :], otv, [("otl", ok)], [("out", s, t)])
                    P.barrier()
                    chk("p4")

        except _Stop:
            pass
        P.barrier()
        P.emit()
    return nc


_CACHE = {}


def kernel(**inputs):
    x = np.ascontiguousarray(np.asarray(inputs["x"], dtype=np.float32))
    nb = x.shape[0]
    nseq = nb // NCORES
    key = ("fused", nseq)
    if key not in _CACHE:
        _CACHE[key] = build_program(nseq)
    nc = _CACHE[key]
    consts = _consts()
    in_maps = []
    for i in range(NCORES):
        m = {"x": x[i * nseq:(i + 1) * nseq]}
        for n in W_NAMES:
            m[n] = np.ascontiguousarray(np.asarray(inputs[n], dtype=np.float32))
        m.update(consts)
        in_maps.append(m)
    res = run_bass_kernel_spmd(nc, in_maps, core_ids=list(range(NCORES)))
    return np.concatenate([np.asarray(r["out"]) for r in res.results], axis=0).astype(np.float32)
```

```python
import contextlib
import math
import numpy as np
import ml_dtypes
import concourse.bass as bass
import concourse.mybir as mybir
from concourse.bass_utils import run_bass_kernel_spmd

F32 = mybir.dt.float32
BF16 = mybir.dt.bfloat16
AF = mybir.ActivationFunctionType
ALU = mybir.AluOpType
AX = mybir.AxisListType

D = 1024
S = 2048
DEPTH = 2
NCORES = 8
INW = 6816
FFN = 2816
NHC = FFN // 128
EPS = 1e-6
NG = S // 512
NT = S // 128


class Prog:
    ENGS = ("pe", "act", "dve", "pool", "sp")
    SAME_ENG_RAW = ("act", "dve", "pool")

    def __init__(self, nc, stack):
        self.nc = nc
        self.stack = stack
        self.ops = {e: [] for e in self.ENGS}
        self.last_w = {}
        self.readers = {}
        self.seen = {e: {} for e in self.ENGS}
        self.marked = {e: set() for e in self.ENGS}
        self.dma_cnt = {}
        self.dma_sem = {}
        self.eng_sem = {}

    def _need(self, consumer, ev, waits):
        if ev is None:
            return
        kind, key, val = ev
        if kind == "e" and key == consumer:
            return
        if self.seen[consumer].get((kind, key), -1) >= val:
            return
        self.seen[consumer][(kind, key)] = val
        waits.append(ev)
        if kind == "e":
            self.marked[key].add(val)

    def _deps(self, consumer, reads, writes, waits):
        for r in reads:
            w = self.last_w.get(r)
            if w is None:
                continue
            if w[0] == "e" and w[1] == consumer:
                if consumer in self.SAME_ENG_RAW and self.seen[consumer].get(("s", consumer), -1) < w[2]:
                    self.seen[consumer][("s", consumer)] = w[2]
                    waits.append(w)
                    self.marked[consumer].add(w[2])
            else:
                self._need(consumer, w, waits)
        for r in writes:
            self._need(consumer, self.last_w.get(r), waits)
            for ev in self.readers.get(r, {}).values():
                self._need(consumer, ev, waits)

    def _record(self, ev, reads, writes):
        for r in reads:
            self.readers.setdefault(r, {})[(ev[0], ev[1])] = ev
        for r in writes:
            self.last_w[r] = ev
            self.readers[r] = {}

    def op(self, eng, fn, reads=(), writes=()):
        ex = [r for r in reads if isinstance(r, tuple) and r[0] == "pb"]
        if ex:
            reads = [r for r in reads if not (isinstance(r, tuple) and r[0] == "pb")]
            writes = list(writes) + ex
        waits = []
        self._deps(eng, reads, writes, waits)
        idx = len(self.ops[eng])
        self.ops[eng].append(("c", fn, waits, None))
        self._record(("e", eng, idx), reads, writes)

    def dma(self, queue, sem, fn, reads=(), writes=()):
        waits = []
        self._deps(queue, reads, writes, waits)
        cnt = self.dma_cnt.get(sem, 0) + 16
        self.dma_cnt[sem] = cnt
        self.ops[queue].append(("d", fn, waits, sem))
        self._record(("d", sem, cnt), reads, writes)

    def barrier(self):
        evs = []
        for e in self.ENGS:
            n = len(self.ops[e])
            k = n - 1
            while k >= 0 and self.ops[e][k][0] != "c":
                k -= 1
            if k >= 0:
                evs.append(("e", e, k))
        for s, c in self.dma_cnt.items():
            evs.append(("d", s, c))
        for e in self.ENGS:
            waits = []
            for ev in evs:
                self._need(e, ev, waits)
            if waits:
                self.ops[e].append(("w", None, waits, None))

    def emit(self):
        nc = self.nc
        engs = {"pe": nc.tensor, "act": nc.scalar, "dve": nc.vector, "pool": nc.gpsimd, "sp": nc.sync}
        for e in self.ENGS:
            self.eng_sem[e] = self.stack.enter_context(nc.semaphore("s_" + e))
        for s in self.dma_cnt:
            self.dma_sem[s] = self.stack.enter_context(nc.semaphore("d_" + s))
        rank = {}
        for e in self.ENGS:
            rank[e] = {idx: i + 1 for i, idx in enumerate(sorted(self.marked[e]))}
        block = self.stack.enter_context(nc.Block())
        starters = {"pe": block.tensor, "act": block.scalar, "dve": block.vector,
                    "pool": block.gpsimd, "sp": block.sync}

        def body(e):
            def _(engine):
                for i, (kind, fn, waits, sem) in enumerate(self.ops[e]):
                    for (k, key, val) in waits:
                        if k == "e":
                            engine.wait_ge(self.eng_sem[key], rank[key][val])
                        else:
                            engine.wait_ge(self.dma_sem[key], val)
                    if kind == "w":
                        continue
                    inst = fn(engine)
                    if kind == "d":
                        inst.then_inc(self.dma_sem[sem], 16)
                    elif i in rank[e]:
                        inst.then_inc(self.eng_sem[e], 1)
            return _

        for e in self.ENGS:
            if self.ops[e]:
                starters[e](body(e))


def _consts():
    n = 8
    sl = 2.0 ** (-8.0 * np.arange(1, n + 1, dtype=np.float64) / n)
    slopes = np.concatenate([sl[0::2], sl[1::2]])
    ki = np.arange(128, dtype=np.float64)[:, None, None]
    dl = np.arange(-3, 16, dtype=np.float64)[None, None, :]
    bias = (slopes[None, :, None] * (ki - 128.0 * dl)).astype(np.float32).reshape(128, 8 * 19)
    kk = np.arange(128)[:, None]
    qq = np.arange(128)[None, :]
    masks = np.zeros((128, 20, 128), np.float32)
    masks[:, 0, :] = (qq >= kk)
    for dlt in range(19):
        d = dlt * 128 + qq - kk
        m = ((d >= 0) & (d <= 128)).astype(np.float32) + ((d >= 0) & (d % 4 == 0) & (d <= 512)) \
            + ((d >= 0) & (d % 16 == 0) & (d <= 2048))
        masks[:, 1 + dlt, :] = m
    half = 16
    inv = 10000.0 ** (-np.arange(half, dtype=np.float32) / half)
    ang = np.arange(S, dtype=np.float32)[None, :] * inv[:, None]
    cos = np.cos(ang).astype(np.float32)
    sin = np.sin(ang).astype(np.float32)
    rope = np.zeros((2, 32, S), np.float32)
    rope[0, :16] = cos
    rope[0, 16:] = cos
    rope[1, :16] = -sin
    rope[1, 16:] = sin
    return {
        "c_bias": bias,
        "c_masks": masks.astype(ml_dtypes.bfloat16).reshape(128, 20 * 128),
        "c_rope": rope,
        "c_identb": np.eye(128, dtype=np.float32).astype(ml_dtypes.bfloat16),
        "c_identf": np.eye(128, dtype=np.float32),
    }


W_NAMES = ["attn_norm", "w_in", "diff_lambda", "diff_norm", "mla_q_norm", "mla_w_uq", "mla_kv_norm",
           "mla_w_ukv", "w_branch_a", "w_branch_b", "w_branch_c", "w_out", "ffn_norm", "w_ffn_gate",
           "w_ffn_up", "w_ffn_down", "final_norm"]
W_SHAPES = {
    "attn_norm": [DEPTH, D], "w_in": [DEPTH, D, INW], "diff_lambda": [DEPTH, 4, 64], "diff_norm": [DEPTH, 128],
    "mla_q_norm": [DEPTH, 384], "mla_w_uq": [DEPTH, 384, 768], "mla_kv_norm": [DEPTH, 256],
    "mla_w_ukv": [DEPTH, 256, 1024], "w_branch_a": [DEPTH, 512, D], "w_branch_b": [DEPTH, 512, D],
    "w_branch_c": [DEPTH, 512, D], "w_out": [DEPTH, D, D], "ffn_norm": [DEPTH, D],
    "w_ffn_gate": [DEPTH, D, FFN], "w_ffn_up": [DEPTH, D, FFN], "w_ffn_down": [DEPTH, FFN, D],
    "final_norm": [D],
}


class _Stop(Exception):
    pass


def build_program(nseq, layers=(0, 1), first=True, last=True, stop_at=None):
    nc = bass.Bass("TRN2", target_bir_lowering=False)
    dram = {}
    if first:
        x_in = nc.dram_tensor("x", [nseq, S, D], F32, kind="ExternalInput").ap()
    else:
        x_in = nc.dram_tensor("xT_in", [nseq, 128, 8, S], F32, kind="ExternalInput").ap()
    for n in W_NAMES:
        dram[n] = nc.dram_tensor(n, W_SHAPES[n], F32, kind="ExternalInput").ap()
    c_bias = nc.dram_tensor("c_bias", [128, 152], F32, kind="ExternalInput").ap()
    c_masks = nc.dram_tensor("c_masks", [128, 20 * 128], BF16, kind="ExternalInput").ap()
    c_rope = nc.dram_tensor("c_rope", [2, 32, S], F32, kind="ExternalInput").ap()
    c_identb = nc.dram_tensor("c_identb", [128, 128], BF16, kind="ExternalInput").ap()
    c_identf = nc.dram_tensor("c_identf", [128, 128], F32, kind="ExternalInput").ap()
    if last:
        out = nc.dram_tensor("out", [nseq, S, D], F32, kind="ExternalOutput").ap()
    else:
        out = nc.dram_tensor("xT_out", [nseq, 128, 8, S], F32, kind="ExternalOutput").ap()
    xTs = nc.dram_tensor("xTs", [nseq, 128, 8, S], F32).ap()
    NL = len(layers)
    WS = {
        "A": nc.dram_tensor("ws_A", [NL, 2, 128, 8 * 768], BF16).ap(),
        "B": nc.dram_tensor("ws_B", [NL, 2, 128, 8 * 768], BF16).ap(),
        "Cs": nc.dram_tensor("ws_Cs", [NL, 128, 8 * 832], BF16).ap(),
        "Cu": nc.dram_tensor("ws_Cu", [NL, 4, 128, 1664], BF16).ap(),
        "GB": nc.dram_tensor("ws_GB", [NL, 8, 128, 4608], BF16).ap(),
        "O": nc.dram_tensor("ws_O", [NL, 128, 8 * 1024], BF16).ap(),
        "GU": nc.dram_tensor("ws_GU", [NL, NHC, 128, 2048], BF16).ap(),
        "Dn": nc.dram_tensor("ws_Dn", [NL, 8, 128, NHC * 128], BF16).ap(),
    }

    with contextlib.ExitStack() as st:
        def sb(name, shape, dt):
            return st.enter_context(nc.sbuf_tensor(name, shape, dt))

        P = Prog(nc, st)
        identb = sb("identb", [128, 128], BF16)
        identf = sb("identf", [128, 128], F32)
        biasT = sb("biasT", [128, 152], F32)
        masks = sb("masks", [128, 20, 128], BF16)
        mstrip = masks[:, 1:20, :].rearrange("p a b -> p (a b)")
        gains = sb("gains", [128, 64], F32)
        lamt = sb("lamt", [128, 16], F32)
        ones_b = sb("ones_b", [128, 2], BF16)
        ones_r = sb("ones_r", [1, 128], F32)
        srow = [sb(f"srow{i}", [1, 512], F32) for i in range(2)]
        R1 = sb("R1", [128, 8 * S], BF16)
        R2 = sb("R2", [128, 12 * S], BF16)
        R3 = sb("R3", [128, 8 * S], BF16)
        R4 = sb("R4", [128, 20480], BF16)
        WB = [sb(f"WB{i}", [128, 8192], BF16) for i in range(2)]
        pb = [st.enter_context(nc.psum_tensor(f"pb{i}", [128, 512], F32)) for i in range(8)]

        hT = R1[:].rearrange("p (c t) -> p c t", c=8)
        oT = R2[:].rearrange("p (c t) -> p c t", c=12)
        actT = R2[:, 0:NHC * 1024].rearrange("p (c t) -> p c t", c=NHC)
        mixT = R3[:].rearrange("p (c t) -> p c t", c=8)
        QT = R3[:, 0:4096].rearrange("p (h t) -> p h t", h=2)
        KT = R3[:, 4096:8192].rearrange("p (h t) -> p h t", h=2)
        VA = R3[:, 8192:8192 + NT * 260].rearrange("p (t n) -> p t n", t=NT)

        def r4f(off, n):
            return R4[:, off:off + n].bitcast(F32)

        def r4b(off, n):
            return R4[:, off:off + n]

        xg = [r4f(0, 8192).rearrange("p (c t) -> p c t", c=8), r4f(8192, 8192).rearrange("p (c t) -> p c t", c=8)]
        T16 = r4b(16384, 4096)

        bank_ctr = [0]

        def nbank():
            b = bank_ctr[0] % 8
            bank_ctr[0] += 1
            return b

        def MM(o, l, r, start, stop, reads, writes):
            P.op("pe", lambda e: e.matmul(o, lhsT=l, rhs=r, start=start, stop=stop), reads, writes)

        def TR(o, i, ident, reads, writes):
            P.op("pe", lambda e: e.transpose(out=o, in_=i, identity=ident), reads, writes)

        def ACT(o, i, func, reads, writes, bias=None, scale=None, accum=None):
            kw = {}
            if bias is not None:
                kw["bias"] = bias
            if scale is not None:
                kw["scale"] = scale
            if accum is not None:
                kw["accum_out"] = accum
            P.op("act", lambda e: e.activation(out=o, in_=i, func=func, **kw), reads, writes)

        def TS(eng, o, i, s1, s2, op0, op1, reads, writes):
            if op1 is None:
                P.op(eng, lambda e: e.tensor_scalar(out=o, in0=i, scalar1=s1, scalar2=None, op0=op0), reads, writes)
            else:
                P.op(eng, lambda e: e.tensor_scalar(out=o, in0=i, scalar1=s1, scalar2=s2, op0=op0, op1=op1), reads, writes)

        def TT(eng, o, a, b, op, reads, writes):
            P.op(eng, lambda e: e.tensor_tensor(out=o, in0=a, in1=b, op=op), reads, writes)

        def STT(o, a, s, b, op0, op1, reads, writes):
            P.op("dve", lambda e: e.scalar_tensor_tensor(out=o, in0=a, scalar=s, in1=b, op0=op0, op1=op1), reads, writes)

        def CP(eng, o, i, reads, writes):
            if eng == "act":
                P.op("act", lambda e: e.copy(out=o, in_=i), reads, writes)
            else:
                P.op(eng, lambda e: e.tensor_copy(out=o, in_=i), reads, writes)

        def DMA(q, sem, o, i, reads, writes, slow=False):
            if slow:
                P.dma(q, sem, lambda e: e.dma_start(out=o, in_=i, allow_slow_non_contiguous=True), reads, writes)
            else:
                P.dma(q, sem, lambda e: e.dma_start(out=o, in_=i), reads, writes)

        ev_ctr = [0]

        def evac_eng():
            ev_ctr[0] += 1
            return "act" if ev_ctr[0] % 2 else "dve"

        def chk(name):
            if stop_at == name:
                raise _Stop()

        try:
            DMA("sp", "c0", identb[:], c_identb, [], ["const"])
            DMA("sp", "c0", identf[:], c_identf, [], ["const"])
            DMA("sp", "c0", biasT[:], c_bias, [], ["const"])
            DMA("sp", "c0", masks[:].rearrange("p a b -> p (a b)"), c_masks, [], ["const"])
            for li, l in enumerate(layers):
                DMA("sp", "c0", gains[:, li * 8:li * 8 + 8], dram["attn_norm"][l].rearrange("(c p) -> p c", p=128), [], ["const"], slow=True)
                DMA("sp", "c0", gains[:, 16 + li * 8:16 + li * 8 + 8], dram["ffn_norm"][l].rearrange("(c p) -> p c", p=128), [], ["const"], slow=True)
                DMA("sp", "c0", gains[:, 40 + li * 3:40 + li * 3 + 3], dram["mla_q_norm"][l].rearrange("(c p) -> p c", p=128), [], ["const"], slow=True)
                DMA("sp", "c0", gains[:, 46 + li * 2:46 + li * 2 + 2], dram["mla_kv_norm"][l].rearrange("(c p) -> p c", p=128), [], ["const"], slow=True)
                DMA("sp", "c0", gains[:, 50 + li:51 + li], dram["diff_norm"][l].rearrange("(c p) -> p c", p=128), [], ["const"], slow=True)
            DMA("sp", "c0", gains[:, 32:40], dram["final_norm"].rearrange("(c p) -> p c", p=128), [], ["const"], slow=True)
            P.op("pool", lambda e: e.memset(ones_b[:], 1.0), [], ["const"])
            P.op("pool", lambda e: e.memset(ones_r[:], 1.0), [], ["const"])
            dlt = r4f(0, 2 * DEPTH * 256)
            DMA("sp", "c0", dlt, dram["diff_lambda"].rearrange("l a b -> (l a b)").partition_broadcast(128), [], ["dl"])
            P.barrier()
            prodt = r4f(4096, 128)
            for li, l in enumerate(layers):
                lam_init = 0.8 - 0.6 * math.exp(-0.3 * l)
                for k in range(2):
                    a = dlt[:, l * 256 + k * 128:l * 256 + k * 128 + 64]
                    b = dlt[:, l * 256 + k * 128 + 64:l * 256 + k * 128 + 128]
                    TT("dve", prodt, a, b, ALU.mult, ["dl"], ["prodt"])
                    P.op("dve", lambda e, o=lamt[:, li * 4 + 3:li * 4 + 4]: e.tensor_reduce(out=o, in_=prodt, axis=AX.X, op=ALU.add),
                         ["prodt"], ["lamj"])
                    ACT(lamt[:, li * 4 + k:li * 4 + k + 1], lamt[:, li * 4 + 3:li * 4 + 4], AF.Exp, ["lamj"], [("lame", k)])
                TT("dve", lamt[:, li * 4 + 3:li * 4 + 4], lamt[:, li * 4 + 1:li * 4 + 2], lamt[:, li * 4:li * 4 + 1], ALU.subtract,
                   [("lame", 0), ("lame", 1)], ["lamj"])
                TS("dve", lamt[:, li * 4 + 2:li * 4 + 3], lamt[:, li * 4 + 3:li * 4 + 4], -lam_init, None, ALU.add, None, ["lamj"], ["const"])
                TS("dve", gains[:, 52 + li:53 + li], gains[:, 50 + li:51 + li], 1.0 - lam_init, None, ALU.mult, None, ["const"], ["const2"])
            P.barrier()
            chk("consts")

            stg = [R2[:, i * 12288:(i + 1) * 12288].bitcast(F32) for i in range(2)]
            cst = [R3[:, i * 6144:(i + 1) * 6144] for i in range(2)]
            pp_ctr = [0]

            def prep(dst, n, pieces):
                k = pp_ctr[0] % 2
                pp_ctr[0] += 1
                for (vf, src) in pieces:
                    DMA("sp", f"pp{k}", vf(stg[k]), src, [], [("stg", k)])
                eng = ("dve", "pool")[(pp_ctr[0] // 2) % 2]
                CP(eng, cst[k][:, 0:n], stg[k][:, 0:n], [("stg", k)], [("cst", k)])
                DMA("act", f"pq{k}", dst, cst[k][:, 0:n], [("cst", k)], ["wscr"])

            def v3(c, n, lo, hi):
                return lambda t: t[:, 0:c * n].rearrange("p (c n) -> p c n", c=c)[:, :, lo:hi]

            for li, l in enumerate(layers):
                win = dram["w_in"][l]

                def wcols(lo, n):
                    return win[:, lo:lo + n].rearrange("(c p) n -> p c n", p=128)

                for u in range(2):
                    prep(WS["A"][li, u], 6144, [(v3(8, 768, k * 256, (k + 1) * 256), wcols(k * 512 + u * 256, 256)) for k in range(3)])
                    prep(WS["B"][li, u], 6144, [(v3(8, 768, k * 256, (k + 1) * 256), wcols(1536 + k * 512 + u * 256, 256)) for k in range(3)])
                for hfc in range(2):
                    def wch(lo, n, hfc=hfc):
                        return win[hfc * 512:(hfc + 1) * 512, lo:lo + n].rearrange("(c p) n -> p c n", p=128)
                    prep(WS["Cs"][li][:, hfc * 3328:(hfc + 1) * 3328], 3328, [
                        (v3(4, 832, 0, 640), wch(3072, 640)),
                        (v3(4, 832, 640, 736), wch(3072 + 384 + 192, 96)),
                        (v3(4, 832, 736, 800), wch(3072 + 384 + 192, 64)),
                        (v3(4, 832, 800, 816), wch(3072 + 384 + 256 + 16, 16)),
                        (v3(4, 832, 816, 832), wch(3072 + 384 + 256, 16)),
                    ])
                uq = dram["mla_w_uq"][l]
                ukv = dram["mla_w_ukv"][l]
                for u in range(4):
                    pcs = []
                    pcs.append((lambda t: t[:, 0:576].rearrange("p (c n) -> p c n", c=3),
                                uq[:, u * 192:(u + 1) * 192].rearrange("(c p) n -> p c n", p=128)))
                    for hh in range(2):
                        base = (2 * u + hh) * 96
                        pcs.append((lambda t, hh=hh: t[:, 576:1152].rearrange("p (c n) -> p c n", c=3)[:, :, hh * 96:hh * 96 + 64],
                                    uq[:, base:base + 64].rearrange("(c p) n -> p c n", p=128)))
                        pcs.append((lambda t, hh=hh: t[:, 576:1152].rearrange("p (c n) -> p c n", c=3)[:, :, hh * 96 + 64:hh * 96 + 80],
                                    uq[:, base + 80:base + 96].rearrange("(c p) n -> p c n", p=128)))
                        pcs.append((lambda t, hh=hh: t[:, 576:1152].rearrange("p (c n) -> p c n", c=3)[:, :, hh * 96 + 80:hh * 96 + 96],
                                    uq[:, base + 64:base + 80].rearrange("(c p) n -> p c n", p=128)))
                    pcs.append((lambda t: t[:, 1152:1664].rearrange("p (c n) -> p c n", c=2),
                                ukv[:, u * 256:(u + 1) * 256].rearrange("(c p) n -> p c n", p=128)))
                    prep(WS["Cu"][li, u], 1664, pcs)
                for n in range(8):
                    pcs = []
                    for m in range(3):
                        pcs.append((lambda t, m=m: t[:, 0:3072].rearrange("p (c m n) -> p c m n", c=8, m=3)[:, :, m, :],
                                    wcols(3744 + m * 1024 + n * 128, 128)))
                        wbr = dram[("w_branch_a", "w_branch_b", "w_branch_c")[m]][l]
                        pcs.append((lambda t, m=m: t[:, 3072:4608].rearrange("p (c m n) -> p c m n", c=4, m=3)[:, :, m, :],
                                    wbr[:, n * 128:(n + 1) * 128].rearrange("(c p) n -> p c n", p=128)))
                    prep(WS["GB"][li, n], 4608, pcs)
                wo = dram["w_out"][l]
                for hf in range(2):
                    prep(WS["O"][li][:, hf * 4096:(hf + 1) * 4096], 4096,
                         [(lambda t: t[:, 0:4096].rearrange("p (c n) -> p c n", c=4),
                           wo[hf * 512:(hf + 1) * 512, :].rearrange("(c p) n -> p c n", p=128))])
                wg = dram["w_ffn_gate"][l]
                wu = dram["w_ffn_up"][l]
                for hc in range(NHC):
                    prep(WS["GU"][li, hc], 2048, [
                        (lambda t: t[:, 0:2048].rearrange("p (c k n) -> p c k n", c=8, k=2)[:, :, 0, :],
                         wg[:, hc * 128:(hc + 1) * 128].rearrange("(c p) n -> p c n", p=128)),
                        (lambda t: t[:, 0:2048].rearrange("p (c k n) -> p c k n", c=8, k=2)[:, :, 1, :],
                         wu[:, hc * 128:(hc + 1) * 128].rearrange("(c p) n -> p c n", p=128)),
                    ])
                wd = dram["w_ffn_down"][l]
                for n in range(8):
                    prep(WS["Dn"][li, n], NHC * 128, [
                        (lambda t: t[:, 0:NHC * 128].rearrange("p (c n) -> p c n", c=NHC),
                         wd[:, n * 128:(n + 1) * 128].rearrange("(c p) n -> p c n", p=128))])
            P.barrier()
            chk("prepass")

            wslot = [0]

            def load_w(src, n):
                k = wslot[0] % 2
                wslot[0] += 1
                DMA("sp", f"wb{k}", WB[k][:, 0:n], src, ["wscr"], [("WB", k)])
                return k

            def fm_rstd(sq_chunks, nfeat, reads):
                b1 = nbank()
                nck = len(sq_chunks)
                for c, sq in enumerate(sq_chunks):
                    MM(pb[b1][0:1, :], ones_b[:, 0:1], sq, c == 0, c == nck - 1, reads + ["const"], [("pb", b1)])
                k = b1 % 2
                TS("dve", srow[k][:], pb[b1][0:1, :], 1.0 / nfeat, EPS, ALU.mult, ALU.add, [("pb", b1)], [("srow", k)])
                ACT(srow[k][:], srow[k][:], AF.Ln, [("srow", k)], [("srow", k)])
                ACT(srow[k][:], srow[k][:], AF.Exp, [("srow", k)], [("srow", k)], scale=-0.5)
                b2 = nbank()
                MM(pb[b2][:], ones_r[0:1, :], srow[k][:], True, True, [("srow", k), "const"], [("pb", b2)])
                return b2

            for s in range(nseq):
                if first:
                    xin = [r4f(16384 + i * 2048, 2048) for i in range(2)]
                    for t in range(NT):
                        g, tt = divmod(t, 4)
                        k = t % 2
                        DMA("sp", f"xin{k}", xin[k], x_in[s, t * 128:(t + 1) * 128, :], [], [("xin", k)])
                        for hf in range(2):
                            b = nbank()
                            for c4 in range(4):
                                c = hf * 4 + c4
                                TR(pb[b][:, c4 * 128:(c4 + 1) * 128], xin[k][:, c * 128:(c + 1) * 128], identf[:],
                                   [("xin", k), "const"], [("pb", b)])
                            CP(evac_eng(), xg[g % 2][:, hf * 4:hf * 4 + 4, tt * 128:(tt + 1) * 128],
                               pb[b][:].rearrange("p (c t) -> p c t", c=4), [("pb", b)], [("xg", g % 2)])
                        if tt == 3:
                            DMA("sp", f"xst{g % 2}", xTs[s, :, :, g * 512:(g + 1) * 512], xg[g % 2], [("xg", g % 2)], [("xTs", s, g)])
                else:
                    for g in range(NG):
                        DMA("sp", f"xg{g % 2}", xg[g % 2], x_in[s, :, :, g * 512:(g + 1) * 512], [], [("xg", g % 2)])
                        DMA("sp", f"xst{g % 2}", xTs[s, :, :, g * 512:(g + 1) * 512], xg[g % 2], [("xg", g % 2)], [("xTs", s, g)])
                P.barrier()
                chk("stage0")

                for li, l in enumerate(layers):
                    is_last_layer = last and (li == NL - 1)
                    sqv = T16.rearrange("p (c t) -> p c t", c=8)
                    kcs = load_w(WS["Cs"][li], 8 * 832)
                    for g in range(NG):
                        k = g % 2
                        DMA("sp", f"xg{k}", xg[k], xTs[s, :, :, g * 512:(g + 1) * 512], [("xTs", s, g)], [("xg", k)])
                        ACT(sqv, xg[k], AF.Square, [("xg", k)], ["sqv"])
                        b2 = fm_rstd([sqv[:, c, :] for c in range(8)], D, ["sqv"])
                        for c in range(8):
                            STT(hT[:, c, g * 512:(g + 1) * 512], xg[k][:, c, :], gains[:, li * 8 + c:li * 8 + c + 1], pb[b2][:],
                                ALU.mult, ALU.mult, [("xg", k), ("pb", b2), "const"], [("hT", g)])
                    P.barrier()
                    chk("p1")

                    wcs = WB[kcs][:, 0:8 * 832].rearrange("p (c n) -> p c n", c=8)
                    craw = r4f(0, 5 * 1024).rearrange("p (c t) -> p c t", c=5)
                    csq = r4b(10240, 5 * 512).rearrange("p (c t) -> p c t", c=5)
                    ropet = [r4f(12800 + i * 2048, 2048).rearrange("p (a t) -> p a t", a=2) for i in range(2)]
                    rtmp = [r4f(16896 + i * 1024, 1024) for i in range(2)]
                    for g in range(NG):
                        gs = slice(g * 512, (g + 1) * 512)
                        for cc in range(5):
                            b = nbank()
                            for c in range(8):
                                MM(pb[b][:], wcs[:, c, cc * 128:(cc + 1) * 128], hT[:, c, gs], c == 0, c == 7,
                                   [("hT", g), ("WB", kcs)], [("pb", b)])
                            CP("dve", craw[:, cc, :], pb[b][:], [("pb", b)], [("craw", cc)])
                            ACT(csq[:, cc, :], pb[b][:], AF.Square, [("pb", b)], [("csq", cc)])
                        chk("a_proj")
                        for (c0, nch, nfeat, gcol, dst0) in ((0, 3, 384, 40 + li * 3, 0), (3, 2, 256, 46 + li * 2, 3)):
                            b2 = fm_rstd([csq[:, c0 + c, :] for c in range(nch)], nfeat, [("csq", c0 + c) for c in range(nch)])
                            for c in range(nch):
                                STT(oT[:, dst0 + c, gs], craw[:, c0 + c, :], gains[:, gcol + c:gcol + c + 1], pb[b2][:],
                                    ALU.mult, ALU.mult, [("craw", c0 + c), ("pb", b2), "const"], [("oT", dst0 + c, g)])
                        chk("a_norm")
                        rk = g % 2
                        DMA("sp", f"rope{rk}", ropet[rk][64:96, :, :], c_rope[:, :, gs].rearrange("a p t -> p a t"), [], [("rope", rk)])
                        bm = nbank()
                        for c in range(8):
                            MM(pb[bm][0:96, :], wcs[:, c, 640:736], hT[:, c, gs], c == 0, c == 7, [("hT", g), ("WB", kcs)], [("pb", bm)])
                        bs = nbank()
                        for c in range(8):
                            MM(pb[bs][0:96, :], wcs[:, c, 736:832], hT[:, c, gs], c == 0, c == 7, [("hT", g), ("WB", kcs)], [("pb", bs)])
                        chk("a_mm")
                        TT("dve", rtmp[0][64:96, :], pb[bm][64:96, :], ropet[rk][64:96, 0, :], ALU.mult, [("pb", bm), ("rope", rk)], [("rtmp", 0)])
                        TT("dve", rtmp[1][64:96, :], pb[bs][64:96, :], ropet[rk][64:96, 1, :], ALU.mult, [("pb", bs), ("rope", rk)], [("rtmp", 1)])
                        chk("a_tt")
                        TT("pool", oT[64:96, 5, gs], rtmp[0][64:96, :], rtmp[1][64:96, :], ALU.add, [("rtmp", 0), ("rtmp", 1)], [("oT", 5, g)])
                        chk("a_pool")

                    PT = [r4b(18944 + i * 512, 512) for i in range(3)]
                    o_tm = r4b(0, 1024).rearrange("p (q n) -> p q n", q=4)
                    t0 = r4f(1024, 1024).rearrange("p (q n) -> p q n", q=4)
                    t1 = r4f(2048, 1024).rearrange("p (q n) -> p q n", q=4)
                    ofp = r4f(3072, 1024).rearrange("p (q n) -> p q n", q=4)
                    junk = r4f(4096, 256)
                    small = r4f(4352, 64)
                    ropeu = [r4f(4480 + i * 2048, 2048).rearrange("p (a t) -> p a t", a=2) for i in range(2)]
                    rtu = [r4f(8576 + i * 1024, 1024) for i in range(2)]
                    units = [("C", u) for u in range(4)] + [("A", u) for u in range(2)] + [("B", u) for u in range(2)]

                    def unit_src(kind, u):
                        if kind == "C":
                            return WS["Cu"][li, u], 1664
                        return WS[kind][li, u], 6144

                    P.barrier()
                    chk("p2a")
                    knext = load_w(*unit_src(*units[0]))
                    for ui, (kind, u) in enumerate(units):
                        kw = knext
                        if ui + 1 < len(units):
                            knext = load_w(*unit_src(*units[ui + 1]))
                        dv = 64 if kind == "C" else 128
                        dva = dv + 1
                        P.op("pool", lambda e, dva=dva: e.memset(VA[:, :, 0:2 * dva].rearrange("p t (h n) -> p t h n", h=2)[:, :, :, dva - 1:dva], 1.0),
                             [], [("VA", t) for t in range(NT)])
                        if kind in ("A", "B"):
                            wv = WB[kw][:, 0:6144].rearrange("p (c n) -> p c n", c=8)
                            for hh in range(2):
                                for g in range(NG):
                                    gs = slice(g * 512, (g + 1) * 512)
                                    for (dstT, nm, co) in ((QT, "QT", 0), (KT, "KT", 256)):
                                        b = nbank()
                                        for c in range(8):
                                            MM(pb[b][:], wv[:, c, co + hh * 128:co + (hh + 1) * 128], hT[:, c, gs], c == 0, c == 7,
                                               [("hT", g), ("WB", kw)], [("pb", b)])
                                        CP(evac_eng(), dstT[:, hh, gs], pb[b][:], [("pb", b)], [(nm, hh, g)])
                            for t in range(NT):
                                b = nbank()
                                for c in range(8):
                                    MM(pb[b][:, 0:256], hT[:, c, t * 128:(t + 1) * 128], wv[:, c, 512:768], c == 0, c == 7,
                                       [("hT", t // 4), ("WB", kw)], [("pb", b)])
                                CP(evac_eng(), VA[:, t, 0:258].rearrange("p (h n) -> p h n", h=2)[:, :, 0:128],
                                   pb[b][:, 0:256].rearrange("p (h n) -> p h n", h=2), [("pb", b)], [("VA", t)])
                        else:
                            wm = WB[kw][:, 0:576].rearrange("p (c n) -> p c n", c=3)
                            wsw = WB[kw][:, 576:1152].rearrange("p (c n) -> p c n", c=3)
                            wkv = WB[kw][:, 1152:1664].rearrange("p (c n) -> p c n", c=2)
                            for g in range(NG):
                                gs = slice(g * 512, (g + 1) * 512)
                                rk = g % 2
                                DMA("sp", f"ropeu{rk}", ropeu[rk][64:96, :, :], c_rope[:, :, gs].rearrange("a p t -> p a t"), [], [("ropeu", rk)])
                                for hh in range(2):
                                    bm = nbank()
                                    for c in range(3):
                                        MM(pb[bm][0:96, :], wm[:, c, hh * 96:(hh + 1) * 96], oT[:, c, gs], c == 0, c == 2,
                                           [("oT", c, g), ("WB", kw)], [("pb", bm)])
                                    bs = nbank()
                                    for c in range(3):
                                        MM(pb[bs][0:96, :], wsw[:, c, hh * 96:(hh + 1) * 96], oT[:, c, gs], c == 0, c == 2,
                                           [("oT", c, g), ("WB", kw)], [("pb", bs)])
                                    CP("act", QT[0:64, hh, gs], pb[bm][0:64, :], [("pb", bm)], [("QT", hh, g)])
                                    TT("dve", rtu[0][64:96, :], pb[bm][64:96, :], ropeu[rk][64:96, 0, :], ALU.mult, [("pb", bm), ("ropeu", rk)], [("rtu", 0)])
                                    TT("dve", rtu[1][64:96, :], pb[bs][64:96, :], ropeu[rk][64:96, 1, :], ALU.mult, [("pb", bs), ("ropeu", rk)], [("rtu", 1)])
                                    TT("pool", QT[64:96, hh, gs], rtu[0][64:96, :], rtu[1][64:96, :], ALU.add, [("rtu", 0), ("rtu", 1)], [("QT", hh, g)])
                                    bk = nbank()
                                    for c in range(2):
                                        MM(pb[bk][0:64, :], wkv[:, c, hh * 128:hh * 128 + 64], oT[:, 3 + c, gs], c == 0, c == 1,
                                           [("oT", 3 + c, g), ("WB", kw)], [("pb", bk)])
                                    CP("act", KT[0:64, hh, gs], pb[bk][0:64, :], [("pb", bk)], [("KT", hh, g)])
                                    CP("pool", KT[64:96, hh, gs], oT[64:96, 5, gs], [("oT", 5, g)], [("KT", hh, g)])
                            for t in range(NT):
                                b = nbank()
                                for c in range(2):
                                    MM(pb[b][:, 0:128], oT[:, 3 + c, t * 128:(t + 1) * 128],
                                       wkv[:, c, :].rearrange("p (h n) -> p h n", h=2)[:, :, 64:128], c == 0, c == 1,
                                       [("oT", 3 + c, t // 4), ("WB", kw)], [("pb", b)])
                                CP(evac_eng(), VA[:, t, 0:130].rearrange("p (h n) -> p h n", h=2)[:, :, 0:64],
                                   pb[b][:, 0:128].rearrange("p (h n) -> p h n", h=2), [("pb", b)], [("VA", t)])

                        if kind == "A":
                            pheads = [(hh, c) for hh in range(2) for c in range(2)]
                            scale = 64 ** -0.5
                        elif kind == "B":
                            pheads = [(hh, None) for hh in range(2)]
                            scale = 128 ** -0.5
                        else:
                            pheads = [(hh, None) for hh in range(2)]
                            scale = 96 ** -0.5
                        steps = [(g, pi, j) for g in range(NG) for pi in range(len(pheads)) for j in range(4 * g + 4)]

                        def ph_info(pi):
                            hh, comp = pheads[pi]
                            if kind == "A":
                                return hh, comp, slice(comp * 64, comp * 64 + 64), 2 * u + hh
                            if kind == "B":
                                return hh, comp, slice(0, 128), 4 + 2 * u + hh
                            return hh, comp, slice(0, 96), None

                        def QK(t):
                            g, pi, j = steps[t]
                            hh, comp, rows, hd = ph_info(pi)
                            i0 = max(j, 4 * g)
                            n = (4 * g + 4 - i0) * 128
                            MM(pb[t % 2][:, 0:n], KT[rows, hh, j * 128:(j + 1) * 128], QT[rows, hh, i0 * 128:(4 * g + 4) * 128],
                               True, True, [("KT", hh, j // 4), ("QT", hh, g)], [("pb", t % 2)])

                        QK(0)
                        for t, (g, pi, j) in enumerate(steps):
                            hh, comp, rows, hd = ph_info(pi)
                            i0 = max(j, 4 * g)
                            nq = 4 * g + 4 - i0
                            sl = t % 3
                            ps = pb[t % 2]
                            W = nq * 128
                            steep = hd in (0, 4)
                            if kind == "C":
                                ACT(PT[sl][:, 0:W], ps[:, 0:W], AF.Exp, [("pb", t % 2)], [("PT", sl)], scale=scale)
                            elif not steep:
                                dlt_g = 4 * g - j
                                ACT(PT[sl][:, 0:W], ps[:, 0:W], AF.Exp, [("pb", t % 2)], [("PT", sl)],
                                    bias=biasT[:, hd * 19 + dlt_g + 3:hd * 19 + dlt_g + 4], scale=scale)
                            else:
                                for i in range(i0, 4 * g + 4):
                                    q0 = (i - i0) * 128
                                    ACT(PT[sl][:, q0:q0 + 128], ps[:, q0:q0 + 128], AF.Exp, [("pb", t % 2)], [("PT", sl)],
                                        bias=biasT[:, hd * 19 + (i - j) + 3:hd * 19 + (i - j) + 4], scale=scale)
                            if kind == "B":
                                TT("dve", PT[sl][:, 0:W], PT[sl][:, 0:W], mstrip[:, (i0 - j) * 128:(i0 - j) * 128 + W], ALU.mult,
                                   [("PT", sl), "const"], [("PT", sl)])
                            elif j >= 4 * g:
                                TT("pool", PT[sl][:, 0:128], PT[sl][:, 0:128], masks[:, 0, :], ALU.mult, [("PT", sl), "const"], [("PT", sl)])
                            if t + 1 < len(steps):
                                QK(t + 1)
                            for i in range(i0, 4 * g + 4):
                                q0 = (i - i0) * 128
                                qi = i - 4 * g
                                MM(pb[2 + qi][:, 0:dva], PT[sl][:, q0:q0 + 128], VA[:, j, hh * dva:(hh + 1) * dva], j == 0, j == i,
                                   [("PT", sl), ("VA", j)], [("pb", 2 + qi)])
                            if j != 4 * g + 3:
                                continue
                            for qi in range(4):
                                po = pb[2 + qi]
                                P.op("dve", lambda e, o=small[:, qi:qi + 1], i_=po[:, dv:dva]: e.reciprocal(out=o, in_=i_),
                                     [("pb", 2 + qi)], [("rec", qi)])
                                if kind == "A":
                                    dst = (t0 if comp == 0 else t1)[:, qi, :]
                                    TS("dve", dst, po[:, 0:dv], small[:, qi:qi + 1], None, ALU.mult, None, [("pb", 2 + qi), ("rec", qi)], [("tc", comp, qi)])
                                else:
                                    TS("dve", o_tm[:, qi, hh * dv:(hh + 1) * dv], po[:, 0:dv], small[:, qi:qi + 1], None, ALU.mult, None,
                                       [("pb", 2 + qi), ("rec", qi)], [("otm", qi)])
                            if kind == "A" and comp == 1:
                                for qi in range(4):
                                    STT(ofp[:, qi, :], t1[:, qi, :], lamt[:, li * 4 + 2:li * 4 + 3], t0[:, qi, :], ALU.mult, ALU.add,
                                        [("tc", 0, qi), ("tc", 1, qi), "const"], [("ofp", qi)])
                                    ACT(junk, ofp[:, qi, :], AF.Square, [("ofp", qi)], ["junk", ("ssq", qi)], accum=small[:, 4 + qi:5 + qi])
                                TS("dve", small[:, 8:12], small[:, 4:8], 1.0 / 128, EPS, ALU.mult, ALU.add, [("ssq", q) for q in range(4)], ["rstdA"])
                                ACT(small[:, 8:12], small[:, 8:12], AF.Ln, ["rstdA"], ["rstdA"])
                                ACT(small[:, 8:12], small[:, 8:12], AF.Exp, ["rstdA"], ["rstdA"], scale=-0.5)
                                for qi in range(4):
                                    TS("dve", o_tm[:, qi, hh * 128:(hh + 1) * 128], ofp[:, qi, :], small[:, 8 + qi:9 + qi], None, ALU.mult, None,
                                       [("ofp", qi), "rstdA"], [("otm", qi)])
                            if pi != len(pheads) - 1:
                                continue
                            nchk = 1 if kind == "C" else 2
                            for ck in range(nchk):
                                if kind == "C":
                                    chunk = 8 + u
                                elif kind == "A":
                                    chunk = 2 * u + ck
                                else:
                                    chunk = 4 + 2 * u + ck
                                bt = 6 + (ck + g) % 2
                                ptr = pb[bt][:].bitcast(BF16)
                                for qi in range(4):
                                    TR(ptr[:, qi * 128:(qi + 1) * 128], o_tm[:, qi, ck * 128:(ck + 1) * 128], identb[:],
                                       [("otm", qi), "const"], [("pb", bt)])
                                if kind == "A":
                                    TS("dve", oT[:, chunk, g * 512:(g + 1) * 512], ptr[:, 0:512], gains[:, 52 + li:53 + li], None, ALU.mult, None,
                                       [("pb", bt), "const2"], [("oT", chunk, g)])
                                else:
                                    CP("dve", oT[:, chunk, g * 512:(g + 1) * 512], ptr[:, 0:512], [("pb", bt)], [("oT", chunk, g)])

                    P.barrier()
                    chk("units")
                    sg = [r4f(i * 1024, 1024) for i in range(3)]
                    pr = [r4f(3072 + i * 1024, 1024) for i in range(3)]
                    knext = load_w(WS["GB"][li, 0], 4608)
                    for n in range(8):
                        kw = knext
                        if n + 1 < 8:
                            knext = load_w(WS["GB"][li, n + 1], 4608)
                        else:
                            knext = load_w(WS["O"][li], 8192)
                        wgt = WB[kw][:, 0:3072].rearrange("p (c m n) -> p c m n", c=8, m=3)
                        wbr = WB[kw][:, 3072:4608].rearrange("p (c m n) -> p c m n", c=4, m=3)
                        for g in range(NG):
                            gs = slice(g * 512, (g + 1) * 512)
                            bg, by = [], []
                            for m in range(3):
                                b = nbank()
                                for c in range(8):
                                    MM(pb[b][:], wgt[:, c, m, :], hT[:, c, gs], c == 0, c == 7, [("hT", g), ("WB", kw)], [("pb", b)])
                                bg.append(b)
                                b = nbank()
                                for c in range(4):
                                    MM(pb[b][:], wbr[:, c, m, :], oT[:, m * 4 + c, gs], c == 0, c == 3, [("oT", m * 4 + c, g), ("WB", kw)], [("pb", b)])
                                by.append(b)
                            for m in range(3):
                                ACT(sg[m], pb[bg[m]][:], AF.Sigmoid, [("pb", bg[m])], [("sg", m)])
                                TT("dve", pr[m], sg[m], pb[by[m]][:], ALU.mult, [("sg", m), ("pb", by[m])], [("pr", m)])
                            TT("pool", pr[0], pr[0], pr[1], ALU.add, [("pr", 0), ("pr", 1)], [("pr", 0)])
                            TT("pool", mixT[:, n, gs], pr[0], pr[2], ALU.add, [("pr", 0), ("pr", 2)], [("mixT", n, g)])

                    P.barrier()
                    chk("p3a")
                    kw = knext
                    wo_t = WB[kw][:, 0:8192].rearrange("p (c n) -> p c n", c=8)
                    knext = load_w(WS["GU"][li, 0], 2048)
                    h2T = hT
                    for g in range(NG):
                        gs = slice(g * 512, (g + 1) * 512)
                        k = g % 2
                        DMA("sp", f"xg{k}", xg[k], xTs[s, :, :, gs], [("xTs", s, g)], [("xg", k)])
                        for n in range(8):
                            b = nbank()
                            for c in range(8):
                                MM(pb[b][:], wo_t[:, c, n * 128:(n + 1) * 128], mixT[:, c, gs], c == 0, c == 7, [("mixT", c, g), ("WB", kw)], [("pb", b)])
                            TT("dve", xg[k][:, n, :], xg[k][:, n, :], pb[b][:], ALU.add, [("xg", k), ("pb", b)], [("xg", k)])
                        DMA("act", f"xst{k}", xTs[s, :, :, gs], xg[k], [("xg", k)], [("xTs", s, g)])
                        ACT(sqv, xg[k], AF.Square, [("xg", k)], ["sqv"])
                        b2 = fm_rstd([sqv[:, c, :] for c in range(8)], D, ["sqv"])
                        for c in range(8):
                            STT(h2T[:, c, gs], xg[k][:, c, :], gains[:, 16 + li * 8 + c:16 + li * 8 + c + 1], pb[b2][:],
                                ALU.mult, ALU.mult, [("xg", k), ("pb", b2), "const"], [("h2T", g)])

                    P.barrier()
                    chk("p3b")
                    sqv4 = R3[:, 0:4096].rearrange("p (c t) -> p c t", c=8)
                    silu = [R3[:, 4096 + i * 1024:4096 + (i + 1) * 1024].bitcast(F32) for i in range(2)]
                    otl = [R3[:, 6144 + i * 2048:6144 + (i + 1) * 2048].bitcast(F32) for i in range(2)]
                    for hf in range(2):
                        for hc in range(NHC):
                            kw = knext
                            if hc + 1 < NHC:
                                knext = load_w(WS["GU"][li, hc + 1], 2048)
                            else:
                                knext = load_w(WS["Dn"][li, 0], NHC * 128)
                            wgu = WB[kw][:, 0:2048].rearrange("p (c k n) -> p c k n", c=8, k=2)
                            for g2 in range(2):
                                g = hf * 2 + g2
                                gs = slice(g * 512, (g + 1) * 512)
                                bgt = nbank()
                                for c in range(8):
                                    MM(pb[bgt][:], wgu[:, c, 0, :], h2T[:, c, gs], c == 0, c == 7, [("h2T", g), ("WB", kw)], [("pb", bgt)])
                                bup = nbank()
                                for c in range(8):
                                    MM(pb[bup][:], wgu[:, c, 1, :], h2T[:, c, gs], c == 0, c == 7, [("h2T", g), ("WB", kw)], [("pb", bup)])
                                sk = (hc * 2 + g2) % 2
                                ACT(silu[sk], pb[bgt][:], AF.Silu, [("pb", bgt)], [("silu", sk)])
                                TT("dve", actT[:, hc, g2 * 512:(g2 + 1) * 512], silu[sk], pb[bup][:], ALU.mult, [("silu", sk), ("pb", bup)],
                                   [("actT", hc, g2)])
                        for g2 in range(2):
                            g = hf * 2 + g2
                            DMA("sp", f"xg{g2}", xg[g2], xTs[s, :, :, g * 512:(g + 1) * 512], [("xTs", s, g)], [("xg", g2)])
                        for n in range(8):
                            kw = knext
                            if n + 1 < 8:
                                knext = load_w(WS["Dn"][li, n + 1], NHC * 128)
                            elif hf == 0:
                                knext = load_w(WS["GU"][li, 0], 2048)
                            wdn = WB[kw][:, 0:NHC * 128].rearrange("p (c n) -> p c n", c=NHC)
                            for g2 in range(2):
                                b = nbank()
                                for hc in range(NHC):
                                    MM(pb[b][:], wdn[:, hc, :], actT[:, hc, g2 * 512:(g2 + 1) * 512], hc == 0, hc == NHC - 1,
                                       [("actT", hc, g2), ("WB", kw)], [("pb", b)])
                                TT("dve", xg[g2][:, n, :], xg[g2][:, n, :], pb[b][:], ALU.add, [("xg", g2), ("pb", b)], [("xg", g2)])
                        for g2 in range(2):
                            g = hf * 2 + g2
                            gs = slice(g * 512, (g + 1) * 512)
                            if not is_last_layer:
                                if li == NL - 1:
                                    DMA("act", f"xst{g2}", out[s, :, :, gs], xg[g2], [("xg", g2)], [("outT", s, g)])
                                else:
                                    DMA("act", f"xst{g2}", xTs[s, :, :, gs], xg[g2], [("xg", g2)], [("xTs", s, g)])
                            else:
                                ACT(sqv4, xg[g2], AF.Square, [("xg", g2)], ["sqv4"])
                                b2 = fm_rstd([sqv4[:, c, :] for c in range(8)], D, ["sqv4"])
                                for c in range(8):
                                    STT(xg[g2][:, c, :], xg[g2][:, c, :], gains[:, 32 + c:33 + c], pb[b2][:], ALU.mult, ALU.mult,
                                        [("xg", g2), ("pb", b2), "const"], [("xg", g2)])
                                for tt in range(4):
                                    t = g * 4 + tt
                                    ok = tt % 2
                                    otv = otl[ok]
                                    for hf2 in range(2):
                                        b = nbank()
                                        for c4 in range(4):
                                            c = hf2 * 4 + c4
                                            TR(pb[b][:, c4 * 128:(c4 + 1) * 128], xg[g2][:, c, tt * 128:(tt + 1) * 128], identf[:],
                                               [("xg", g2), "const"], [("pb", b)])
                                        CP(evac_eng(), otv[:, hf2 * 512:(hf2 + 1) * 512], pb[b][:], [("pb", b)], [("otl", ok)])
                                    DMA("sp", f"ost{ok}", out[s, t * 128:(t + 1) * 128, :], otv, [("otl", ok)], [("out", s, t)])
                    P.barrier()
                    chk("p4")

        except _Stop:
            pass
        P.barrier()
        P.emit()
    return nc


_CACHE = {}


def kernel(**inputs):
    x = np.ascontiguousarray(np.asarray(inputs["x"], dtype=np.float32))
    nb = x.shape[0]
    nseq = nb // NCORES
    key = ("fused", nseq)
    if key not in _CACHE:
        _CACHE[key] = build_program(nseq)
    nc = _CACHE[key]
    consts = _consts()
    in_maps = []
    for i in range(NCORES):
        m = {"x": x[i * nseq:(i + 1) * nseq]}
        for n in W_NAMES:
            m[n] = np.ascontiguousarray(np.asarray(inputs[n], dtype=np.float32))
        m.update(consts)
        in_maps.append(m)
    res = run_bass_kernel_spmd(nc, in_maps, core_ids=list(range(NCORES)))
    return np.concatenate([np.asarray(r["out"]) for r in res.results], axis=0).astype(np.float32)
```
